# Optimizing a Trainium2 kernel written in Bass

```python
import jax, jax.numpy as jnp
from jax import lax
import numpy as np

D_MODEL = 4096
BATCH = 4
SEQ = 4096
DEPTH = 1

HEAD_DIM = 128
N_Q_HEADS = D_MODEL // 256
N_KV_HEADS = N_Q_HEADS // 4
Q_PER_KV = N_Q_HEADS // N_KV_HEADS
ATTN_WIDTH = N_Q_HEADS * HEAD_DIM
KV_WIDTH = N_KV_HEADS * HEAD_DIM
WINDOW = 128
BLOCK = 128
ROT_DIM = HEAD_DIM // 4
ROPE_THETA = 500000.0

SGU_CHUNK = 128
SGU_GROUPS = D_MODEL // 256
SGU_GROUP_WIDTH = 128
SGU_WIDTH = SGU_GROUPS * SGU_GROUP_WIDTH

FFN_HIDDEN = -(-(8 * D_MODEL) // (3 * 256)) * 256

IN_WIDTH = ATTN_WIDTH + 2 * KV_WIDTH + 2 * SGU_WIDTH + 2 * D_MODEL
N_MOD = 6
RMS_EPS = 1e-6
LN_EPS = 1e-5

kernel_name = "hybrid_sgu_swa_sink_adaln_block"


def _rms_norm(x, g):
    xf = x.astype(jnp.float32)
    y = xf * lax.rsqrt(jnp.mean(xf * xf, axis=-1, keepdims=True) + RMS_EPS)
    return (y * g.astype(jnp.float32)).astype(x.dtype)


def _layer_norm(x, g, b):
    xf = x.astype(jnp.float32)
    mu = jnp.mean(xf, axis=-1, keepdims=True)
    xc = xf - mu
    y = xc * lax.rsqrt(jnp.mean(xc * xc, axis=-1, keepdims=True) + LN_EPS)
    return (y * g.astype(jnp.float32) + b.astype(jnp.float32)).astype(x.dtype)


def _modulate(h, shift, scale):
    return h * (1 + scale[:, None, :]) + shift[:, None, :]


def _rope_tables(positions, dtype):
    inv_freq = ROPE_THETA ** (-jnp.arange(0, ROT_DIM, 2, dtype=jnp.float32) / ROT_DIM)
    ang = positions.astype(jnp.float32)[..., None] * inv_freq
    return jnp.cos(ang)[:, :, None, :].astype(dtype), jnp.sin(ang)[:, :, None, :].astype(dtype)


def _partial_rope(t, cos, sin):
    half = ROT_DIM // 2
    x1, x2, rest = t[..., :half], t[..., half:ROT_DIM], t[..., ROT_DIM:]
    return jnp.concatenate([x1 * cos - x2 * sin, x2 * cos + x1 * sin, rest], axis=-1)


def _sliding_window_attention(q, k, v, sinks):
    B, S = q.shape[0], q.shape[1]
    nb = S // BLOCK
    qb = q.reshape(B, nb, BLOCK, N_KV_HEADS, Q_PER_KV, HEAD_DIM)

    def with_prev(t):
        tb = t.reshape(B, nb, BLOCK, N_KV_HEADS, HEAD_DIM)
        prev = jnp.pad(tb[:, :-1], ((0, 0), (1, 0), (0, 0), (0, 0), (0, 0)))
        return jnp.concatenate([prev, tb], axis=2)

    kk, vv = with_prev(k), with_prev(v)
    s = jnp.einsum('bnqhgd,bnkhd->bnhgqk', qb, kk,
                   preferred_element_type=jnp.float32) * (HEAD_DIM ** -0.5)
    blk = jnp.arange(nb)[:, None, None]
    qi = jnp.arange(BLOCK)[None, :, None]
    ki = jnp.arange(2 * BLOCK)[None, None, :]
    diff = qi + BLOCK - ki
    kpos = (blk - 1) * BLOCK + ki
    valid = (diff >= 0) & (diff < WINDOW) & (kpos >= 0)
    s = jnp.where(valid[:, None, None, :, :], s, -jnp.inf)
    sink = sinks.astype(jnp.float32).reshape(1, 1, N_KV_HEADS, Q_PER_KV, 1, 1)
    m = jnp.maximum(jnp.max(s, axis=-1, keepdims=True), sink)
    p = jnp.exp(s - m)
    denom = jnp.sum(p, axis=-1, keepdims=True) + jnp.exp(sink - m)
    o = jnp.einsum('bnhgqk,bnkhd->bnqhgd', (p / denom).astype(vv.dtype), vv)
    return o.reshape(B, S, ATTN_WIDTH)


def _spatial_gating(u, v, ln_g, ln_b, w_s, b_s):
    B, S = u.shape[0], u.shape[1]
    nc = S // SGU_CHUNK
    vn = _layer_norm(v, ln_g, ln_b).reshape(B, nc, SGU_CHUNK, SGU_GROUPS, SGU_GROUP_WIDTH)
    causal = jnp.tril(jnp.ones((SGU_CHUNK, SGU_CHUNK), dtype=bool))
    w = jnp.where(causal[None], w_s, jnp.zeros_like(w_s))
    mixed = jnp.einsum('gts,bnsgc->bntgc', w, vn) + b_s.T[:, :, None]
    return u * mixed.reshape(B, S, SGU_WIDTH)


def setup_inputs(seed: int = 0) -> dict:
    key = jax.random.key(seed)
    ks = jax.random.split(key, 24)
    f32 = jnp.float32

    def w(k, shape, fan_in, mult=1.0):
        return jax.random.normal(k, shape, f32) * (mult * fan_in ** -0.5)

    def gain(k, shape):
        return 1.0 + 0.02 * jax.random.normal(k, shape, f32)

    L, D = DEPTH, D_MODEL
    return {
        "x": jax.random.normal(ks[0], (BATCH, SEQ, D), f32),
        "c": jax.random.normal(ks[1], (BATCH, D), f32),
        "positions": jnp.broadcast_to(jnp.arange(SEQ, dtype=jnp.int32)[None, :], (BATCH, SEQ)),
        "w_ada": w(ks[2], (L, D, N_MOD * D), D, 0.5),
        "b_ada": 0.02 * jax.random.normal(ks[3], (L, N_MOD * D), f32),
        "g_pre_mix": gain(ks[4], (L, D)),
        "w_in": w(ks[5], (L, D, IN_WIDTH), D),
        "attn_sinks": 0.5 * jax.random.normal(ks[6], (L, N_Q_HEADS), f32),
        "sgu_ln_g": gain(ks[7], (L, SGU_WIDTH)),
        "sgu_ln_b": 0.02 * jax.random.normal(ks[8], (L, SGU_WIDTH), f32),
        "sgu_w": w(ks[9], (L, SGU_GROUPS, SGU_CHUNK, SGU_CHUNK), SGU_CHUNK),
        "sgu_b": 1.0 + 0.02 * jax.random.normal(ks[10], (L, SGU_GROUPS, SGU_CHUNK), f32),
        "w_proj_sgu": w(ks[11], (L, SGU_WIDTH, D), SGU_WIDTH),
        "w_proj_attn": w(ks[12], (L, ATTN_WIDTH, D), ATTN_WIDTH),
        "w_out": w(ks[13], (L, D, D), D),
        "g_post_mix": gain(ks[14], (L, D)),
        "g_pre_ffn": gain(ks[15], (L, D)),
        "w_gate": w(ks[16], (L, D, FFN_HIDDEN), D),
        "w_up": w(ks[17], (L, D, FFN_HIDDEN), D),
        "w_down": w(ks[18], (L, FFN_HIDDEN, D), FFN_HIDDEN),
        "g_post_ffn": gain(ks[19], (L, D)),
    }


def reference(x, c, positions, w_ada, b_ada, g_pre_mix, w_in, attn_sinks, sgu_ln_g, sgu_ln_b,
              sgu_w, sgu_b, w_proj_sgu, w_proj_attn, w_out, g_post_mix, g_pre_ffn,
              w_gate, w_up, w_down, g_post_ffn):
    B, S = x.shape[0], x.shape[1]
    cos, sin = _rope_tables(positions, x.dtype)
    c_act = jax.nn.silu(c)
    o1 = ATTN_WIDTH
    o2 = o1 + KV_WIDTH
    o3 = o2 + KV_WIDTH
    o4 = o3 + SGU_WIDTH
    o5 = o4 + SGU_WIDTH
    o6 = o5 + D_MODEL
    for l in range(DEPTH):
        mod = c_act @ w_ada[l] + b_ada[l]
        sh1, sc1, gt1, sh2, sc2, gt2 = jnp.split(mod, N_MOD, axis=-1)

        h = _modulate(_rms_norm(x, g_pre_mix[l]), sh1, sc1)
        z = h @ w_in[l]
        q, k, v, su, sv, ga, gb = jnp.split(z, [o1, o2, o3, o4, o5, o6], axis=-1)

        q = _partial_rope(q.reshape(B, S, N_Q_HEADS, HEAD_DIM), cos, sin)
        k = _partial_rope(k.reshape(B, S, N_KV_HEADS, HEAD_DIM), cos, sin)
        v = v.reshape(B, S, N_KV_HEADS, HEAD_DIM)
        attn_out = _sliding_window_attention(q, k, v, attn_sinks[l]) @ w_proj_attn[l]

        su = jax.nn.gelu(su, approximate=False)
        sv = jax.nn.gelu(sv, approximate=False)
        sgu_out = _spatial_gating(su, sv, sgu_ln_g[l], sgu_ln_b[l], sgu_w[l], sgu_b[l]) @ w_proj_sgu[l]

        merged = jax.nn.sigmoid(ga) * sgu_out + jax.nn.sigmoid(gb) * attn_out
        y = _rms_norm(merged @ w_out[l], g_post_mix[l])
        x = x + gt1[:, None, :] * y

        h = _modulate(_rms_norm(x, g_pre_ffn[l]), sh2, sc2)
        f = (jax.nn.silu(h @ w_gate[l]) * (h @ w_up[l])) @ w_down[l]
        x = x + gt2[:, None, :] * _rms_norm(f, g_post_ffn[l])
    return x
```

```python
import numpy as np
from contextlib import ExitStack
import concourse.bass as bass
import concourse.mybir as mybir
from concourse.bass_utils import run_bass_kernel_spmd

F32 = mybir.dt.float32
BF16 = mybir.dt.bfloat16
I32 = mybir.dt.int32
AF = mybir.ActivationFunctionType
ALU = mybir.AluOpType
AX = mybir.AxisListType

P = 128
TT = 512
NCH = 4
RMS_EPS = 1e-6
LN_EPS = 1e-5
ROPE_THETA = 500000.0
NEG = -30000.0
SLOT_ELEMS = 4096
NSLOT = 4


class Cfg:
    def __init__(s, D, SEQ, BATCH, NCORES=8):
        s.D = D; s.KC = D // 128; s.NQ = D // 256; s.NKV = s.NQ // 4; s.G = D // 256
        s.SW = s.G * 128; s.AW = s.NQ * 128; s.KVW = s.NKV * 128
        s.HID = -(-(8 * D) // (3 * 256)) * 256; s.HC = s.HID // 128
        s.INW = s.AW + 2 * s.KVW + 2 * s.SW + 2 * D
        s.oq = 0; s.ok = s.AW; s.ov = s.ok + s.KVW; s.osu = s.ov + s.KVW
        s.osv = s.osu + s.SW; s.oga = s.osv + s.SW; s.ogb = s.oga + D
        s.SEQ = SEQ; s.BATCH = BATCH; s.NCORES = NCORES
        s.CPB = NCORES // BATCH
        s.TOK = SEQ // s.CPB; s.NT = s.TOK // TT
        s.NCG = D // 512
        s.NSG = max(1, s.SW // 512)
        s.SGW = min(512, s.SW)


class Buf:
    __slots__ = ("region", "lo", "hi", "last_w", "readers", "name")

    def __init__(self, region, lo, hi, name=""):
        self.region = region; self.lo = lo; self.hi = hi
        self.last_w = None
        self.readers = {}
        self.name = name


class Sched:
    def __init__(self, nc, es):
        self.nc = nc; self.es = es
        self.regions = {}
        self.eng = {}
        for name, h in (("pe", nc.tensor), ("act", nc.scalar), ("dve", nc.vector),
                        ("pool", nc.gpsimd), ("sp", nc.sync)):
            sem = es.enter_context(nc.semaphore("s_" + name))
            self.eng[name] = {"h": h, "sem": sem, "cnt": 0, "seen": {}}
        self.dsems = {}
        self.nwaits = 0

    def buf(self, region, lo, hi, name=""):
        b = Buf(region, lo, hi, name)
        self.regions.setdefault(region, []).append(b)
        return b

    def _conf(self, b):
        return [o for o in self.regions[b.region] if o.lo < b.hi and b.lo < o.hi]

    def _collect(self, engkey, reads, writes, is_dma):
        deps = {}

        def add(tok, kind):
            sem, val, ek = tok
            if (not is_dma) and ek == engkey and engkey == "pe" and kind != "raw":
                return
            k = id(sem)
            if k not in deps or deps[k][1] < val:
                deps[k] = (sem, val)
        for b in reads:
            for o in self._conf(b):
                if o.last_w is not None:
                    add(o.last_w, "raw")
                if b.region == "PS":
                    for ek, (sem, val) in o.readers.items():
                        if ek != engkey:
                            add((sem, val, ek), "rar")
        for b in writes:
            for o in self._conf(b):
                if o.last_w is not None:
                    add(o.last_w, "waw")
                for ek, (sem, val) in o.readers.items():
                    add((sem, val, ek), "war")
        return deps

    def _emit_waits(self, qname, deps):
        e = self.eng[qname]
        for k, (sem, val) in deps.items():
            if e["seen"].get(k, 0) >= val:
                continue
            e["h"].wait_ge(sem, val)
            e["seen"][k] = val
            self.nwaits += 1

    def _update(self, tok, reads, writes):
        sem, val, ek = tok
        for b in reads:
            b.readers[ek] = (sem, val)
        for b in writes:
            for o in self._conf(b):
                if o is not b:
                    o.last_w = None
                    o.readers = {}
            b.last_w = tok
            b.readers = {}

    def op(self, ename, emit, reads=(), writes=()):
        e = self.eng[ename]
        deps = self._collect(ename, reads, writes, False)
        self._emit_waits(ename, deps)
        ins = emit()
        e["cnt"] += 1
        ins.then_inc(e["sem"], 1)
        self._update((e["sem"], e["cnt"], ename), reads, writes)

    def dma(self, qname, semkey, emit, reads=(), writes=()):
        if semkey not in self.dsems:
            sem = self.es.enter_context(self.nc.semaphore("d_%d" % len(self.dsems)))
            self.dsems[semkey] = [sem, 0]
        ds = self.dsems[semkey]
        ek = ("dma", semkey)
        deps = self._collect(ek, reads, writes, True)
        self._emit_waits(qname, deps)
        ins = emit()
        ds[1] += 16
        ins.then_inc(ds[0], 16)
        self._update((ds[0], ds[1], ek), reads, writes)

    def final_wait(self, qname, bufs):
        deps = {}
        for b in bufs:
            for o in self._conf(b):
                if o.last_w is not None:
                    sem, val, ek = o.last_w
                    k = id(sem)
                    if k not in deps or deps[k][1] < val:
                        deps[k] = (sem, val)
        self._emit_waits(qname, deps)


class TV:
    __slots__ = ("ap", "b")

    def __init__(self, ap, b):
        self.ap = ap; self.b = b


class _Stop(Exception):
    pass


def build_program(cfg, stop=None):
    c = cfg
    D, KC, NQ, NKV, G, HC, TOK, NT, NCG = c.D, c.KC, c.NQ, c.NKV, c.G, c.HC, c.TOK, c.NT, c.NCG
    SW, AW, KVW = c.SW, c.AW, c.KVW
    nc = bass.Bass("TRN2", target_bir_lowering=False)
    es = ExitStack()
    S = Sched(nc, es)

    def din(name, shape, dt=F32):
        return nc.dram_tensor(name, list(shape), dt, kind="ExternalInput").ap()

    x_d = din("x", [TOK, D]); xh_d = din("xh", [P, D]); pos_d = din("pos", [32, TOK + P], I32)
    cT_d = din("cT", [P, KC]); wada_d = din("w_ada", [D, 6 * D]); bada_d = din("b_adaT", [P, 6 * KC])
    gpm_d = din("g_pre_mixT", [P, KC]); gpo_d = din("g_post_mixT", [P, KC])
    gpf_d = din("g_pre_ffnT", [P, KC]); gpof_d = din("g_post_ffnT", [P, KC])
    win_d = din("w_in", [D, c.INW]); sink_d = din("sinkB", [P, NQ])
    lng_d = din("ln_gT", [P, G]); lnb_d = din("ln_bT", [P, G])
    sguw_d = din("sgu_w", [G, P, P]); sgub_d = din("sgu_bB", [P, G * P])
    wps_d = din("w_proj_sgu", [SW, D]); wpa_d = din("w_proj_attn", [AW, D]); wout_d = din("w_out", [D, D])
    wg_d = din("w_gate", [D, c.HID]); wu_d = din("w_up", [D, c.HID]); wd_d = din("w_down", [c.HID, D])
    ident_d = din("identF", [P, P]); rt_d = din("rotT", [P, P]); band_d = din("maskB", [P, 256])
    mask0_d = din("mask0", [P, 256]); tri_d = din("tri", [P, P]); invf_d = din("invf", [32, 1])
    out_d = nc.dram_tensor("out", [TOK, D], F32, kind="ExternalOutput").ap()
    ggd_d = nc.dram_tensor("ggd", [2 * P, D], F32, kind="Internal").ap()

    YB = max(NCH * D * 4, 65536 if D >= 4096 else 0)
    off = 0
    def take(n):
        nonlocal off
        o = off; off += (n + 31) // 32 * 32
        return o
    o_vnq = take(max(NCH * SW * 2, NQ * TT * 2))
    o_k = take(NKV * 640 * 2)
    o_v = take(5 * KVW * 2)
    o_cs = take(2 * 640 * 4)
    o_scr = take(14 * 1024)
    o_oT = take(NQ * TT * 2)
    YB = max(YB, off)
    HB_ = KC * TT * 2
    MB = max(KC * TT * 2, D * 2 + D * 4, 2 * 8 * TT * 2)
    UB = max(G * TT * 2, 16384)

    def sb(name, nbytes):
        return es.enter_context(nc.sbuf_tensor(name, [P, (nbytes + 1) // 2], BF16))
    Y_t = sb("Y", YB); H_t = sb("H", HB_); M_t = sb("M", MB); U_t = sb("U", UB)
    R_t = sb("R", NSLOT * SLOT_ELEMS * 2)
    C_t = sb("C", 24 * 1024)
    tens = {"Y": Y_t, "H": H_t, "M": M_t, "U": U_t, "R": R_t, "C": C_t}

    def V(region, lo, shape, dt, name="", npart=P):
        esz = 2 if dt == BF16 else 4
        n = int(np.prod(shape))
        a = tens[region][0:npart, lo // 2:(lo + n * esz) // 2]
        if dt != BF16:
            a = a.bitcast(dt)
        if len(shape) == 2:
            a = a.rearrange("p (a b) -> p a b", b=shape[1])
        elif len(shape) == 3:
            a = a.rearrange("p (a b c) -> p a b c", b=shape[1], c=shape[2])
        return TV(a, S.buf(region, lo, lo + n * esz, name))

    def sub(tv, lo_bytes, nbytes, name=""):
        return S.buf(tv.b.region, tv.b.lo + lo_bytes, tv.b.lo + lo_bytes + nbytes, name)

    coff = 0
    def ctake(shape, dt, name, npart=P):
        nonlocal coff
        esz = 2 if dt == BF16 else 4
        n = int(np.prod(shape)) * esz
        t = V("C", coff, shape, dt, name, npart)
        coff += (n + 31) // 32 * 32
        return t
    identF = ctake([P], F32, "identF"); identB = ctake([P], BF16, "identB"); onesF = ctake([P], F32, "onesF")
    rotB = ctake([P], BF16, "rotB"); maskB = ctake([256], F32, "maskB"); mask0 = ctake([256], F32, "mask0")
    invf = ctake([1], F32, "invf", 32); sinkB = ctake([NQ], F32, "sinkB"); nsink = ctake([NQ], F32, "nsink")
    lng = ctake([G], F32, "lng"); lnb = ctake([G], F32, "lnb")
    WmT = ctake([G, P], BF16, "WmT"); bias2 = ctake([G, P], F32, "bias2")
    cT = ctake([KC], F32, "cT"); cact = ctake([KC], F32, "cact"); cactB = ctake([KC], BF16, "cactB")
    bada = ctake([6 * KC], F32, "bada"); modT = ctake([6 * KC], F32, "modT")
    gpm = ctake([KC], F32, "gpm"); gpo = ctake([KC], F32, "gpo"); gpf = ctake([KC], F32, "gpf"); gpof = ctake([KC], F32, "gpof")
    a1 = ctake([KC], F32, "a1"); a2 = ctake([KC], F32, "a2"); gg1 = ctake([KC], F32, "gg1"); gg2 = ctake([KC], F32, "gg2")
    epsR = ctake([1], F32, "epsR"); epsL = ctake([1], F32, "epsL")
    ssq = ctake([8], F32, "ssq"); rstd = ctake([8], F32, "rstd")
    ssqp = ctake([NCH, NCG], F32, "ssqp")
    stats = ctake([NCH, c.NSG * 6], F32, "stats"); mv = ctake([NCH, 2], F32, "mv")
    lnr = ctake([NCH], F32, "lnr"); lnm = ctake([NCH], F32, "lnm")
    mx = ctake([8], F32, "mx"); negm = ctake([8], F32, "negm"); rs = ctake([8], F32, "rs"); esk = ctake([8], F32, "esk")
    den = ctake([8], F32, "den")
    khalo = ctake([NKV, P], BF16, "khalo"); vhalo = ctake([KVW], BF16, "vhalo")
    assert coff <= 24 * 1024, coff

    PS = []
    for i in range(8):
        t = es.enter_context(nc.psum_tensor("ps%d" % i, [P, 512], F32))
        PS.append(TV(t[:, :], S.buf("PS", i * 2048, (i + 1) * 2048, "ps%d" % i)))

    slots = [V("R", i * SLOT_ELEMS * 2, [SLOT_ELEMS], BF16, "slot%d" % i) for i in range(NSLOT)]
    slot_ctr = [0]

    def load_slab(W, k0, nk, c0, ncols):
        assert nk * ncols <= SLOT_ELEMS
        i = slot_ctr[0] % NSLOT; slot_ctr[0] += 1
        sl = slots[i]
        dst = sl.ap[:, 0:nk * ncols].rearrange("p (a b) -> p a b", b=ncols)
        src = W[k0 * P:(k0 + nk) * P, c0:c0 + ncols].rearrange("(a p) n -> p a n", p=P)
        S.dma("pool", ("slot", i), lambda: nc.gpsimd.dma_start(out=dst, in_=src), reads=[], writes=[sl.b])
        return dst, sl.b

    def ckpt(n):
        if stop is not None and n == stop:
            raise _Stop()

    out_b = [[[S.buf("OUT", ((ti * NCH + ch) * NCG + cg), ((ti * NCH + ch) * NCG + cg) + 1) for cg in range(NCG)]
              for ch in range(NCH)] for ti in range(NT)]
    try:
        def cload(tv, src):
            S.dma("sp", "const", lambda: nc.sync.dma_start(out=tv.ap, in_=src), reads=[], writes=[tv.b])
        cload(identF, ident_d); cload(maskB, band_d); cload(mask0, mask0_d)
        cload(invf, invf_d); cload(sinkB, sink_d); cload(lng, lng_d); cload(lnb, lnb_d)
        cload(cT, cT_d); cload(bada, bada_d); cload(gpm, gpm_d); cload(gpo, gpo_d); cload(gpf, gpf_d); cload(gpof, gpof_d)
        sguw = V("Y", 0, [G, P], F32, "sguw")
        sgub = V("Y", G * P * 4, [G, P], F32, "sgub")
        tri = V("Y", 2 * G * P * 4, [P], F32, "tri")
        wmf = V("Y", 2 * G * P * 4 + 512, [P], F32, "wmf")
        S.dma("sp", "const", lambda: nc.sync.dma_start(out=sguw.ap, in_=sguw_d.rearrange("g t s -> t g s")), writes=[sguw.b])
        cload(sgub, sgub_d.rearrange("p (g t) -> p g t", t=P)); cload(tri, tri_d)
        rotF = V("Y", 2 * G * P * 4 + 1024, [P], F32, "rotF")
        cload(rotF, rt_d)
        csem, ctot = S.dsems["const"]
        for r in S.regions.values():
            for b in r:
                if b.last_w is not None and b.last_w[2] == ("dma", "const"):
                    b.last_w = (csem, ctot, ("dma", "const"))

        dve = nc.vector; act = nc.scalar; pe = nc.tensor

        S.op("dve", lambda: dve.tensor_copy(out=identB.ap, in_=identF.ap), [identF.b], [identB.b])
        S.op("dve", lambda: dve.tensor_copy(out=rotB.ap, in_=rotF.ap), [rotF.b], [rotB.b])
        S.op("dve", lambda: dve.memset(onesF.ap, 1.0), [], [onesF.b])
        S.op("dve", lambda: dve.memset(epsR.ap, RMS_EPS), [], [epsR.b])
        S.op("dve", lambda: dve.memset(epsL.ap, LN_EPS), [], [epsL.b])
        S.op("dve", lambda: dve.tensor_scalar(out=nsink.ap, in0=sinkB.ap, scalar1=-1.0, scalar2=None, op0=ALU.mult),
             [sinkB.b], [nsink.b])
        S.op("act", lambda: act.activation(out=cact.ap, in_=cT.ap, func=AF.Silu), [cT.b], [cact.b])
        ckpt(1)

        for g in range(G):
            pa = PS[(2 * g) % 8]; pb = PS[(2 * g + 1) % 8]
            S.op("pe", lambda: pe.transpose(out=pa.ap[:, 0:P], in_=sguw.ap[:, g, :], identity=identF.ap),
                 [sguw.b, identF.b], [pa.b])
            S.op("dve", lambda: dve.tensor_tensor(out=wmf.ap, in0=pa.ap[:, 0:P], in1=tri.ap, op=ALU.mult),
                 [pa.b, tri.b], [wmf.b])
            S.op("act", lambda: act.copy(out=WmT.ap[:, g, :], in_=wmf.ap), [wmf.b], [WmT.b])
            S.op("pe", lambda: pe.matmul(pb.ap[:, 0:P], lhsT=onesF.ap, rhs=wmf.ap, start=True, stop=True),
                 [onesF.b, wmf.b], [pb.b])
            S.op("dve", lambda: dve.scalar_tensor_tensor(out=bias2.ap[:, g, :], in0=pb.ap[:, 0:P], scalar=lnb.ap[:, g:g + 1],
                                                         in1=sgub.ap[:, g, :], op0=ALU.mult, op1=ALU.add),
                 [pb.b, lnb.b, sgub.b], [bias2.b])

        ckpt(2)
        S.op("dve", lambda: dve.tensor_copy(out=cactB.ap, in_=cact.ap), [cact.b], [cactB.b])
        psm = PS[7]
        nkA = min(KC, SLOT_ELEMS // P)
        for col in range(6 * KC):
            for s0 in range(0, KC, nkA):
                nke = min(nkA, KC - s0)
                slab, slb = load_slab(wada_d, s0, nke, col * P, P)

                def emit():
                    ins = None
                    for kl in range(nke):
                        kc = s0 + kl
                        ins = pe.matmul(psm.ap[:, col:col + 1], lhsT=slab[:, kl, :], rhs=cactB.ap[:, kc:kc + 1],
                                        start=(kc == 0), stop=(kc == KC - 1))
                    return ins
                S.op("pe", emit, [slb, cactB.b], [psm.b])
        S.op("dve", lambda: dve.tensor_tensor(out=modT.ap, in0=psm.ap[:, 0:6 * KC], in1=bada.ap, op=ALU.add),
             [psm.b, bada.b], [modT.b])
        def mk_a(dst, gvec, sc_lo):
            S.op("dve", lambda: dve.scalar_tensor_tensor(out=dst.ap, in0=modT.ap[:, sc_lo:sc_lo + KC], scalar=1.0, in1=gvec.ap,
                                                         op0=ALU.add, op1=ALU.mult), [modT.b, gvec.b], [dst.b])
        mk_a(a1, gpm, KC); mk_a(a2, gpf, 4 * KC)
        S.op("dve", lambda: dve.tensor_tensor(out=gg1.ap, in0=modT.ap[:, 2 * KC:3 * KC], in1=gpo.ap, op=ALU.mult),
             [modT.b, gpo.b], [gg1.b])
        S.op("dve", lambda: dve.tensor_tensor(out=gg2.ap, in0=modT.ap[:, 5 * KC:6 * KC], in1=gpof.ap, op=ALU.mult),
             [modT.b, gpof.b], [gg2.b])
        sh1 = TV(modT.ap[:, 0:KC], modT.b); sh2 = TV(modT.ap[:, 3 * KC:4 * KC], modT.b)
        ckpt(3)

        Ycc = [[V("Y", (ch * D + cg * 512) * 4, [512], F32, "Y%d_%d" % (ch, cg)) for cg in range(NCG)] for ch in range(NCH)]
        Ych = [tens["Y"][:, ch * D * 2:(ch + 1) * D * 2].bitcast(F32) for ch in range(NCH)]
        def Yb(ch):
            return [t.b for t in Ycc[ch]]
        hT = V("H", 0, [KC, TT], BF16, "hT")
        mergedT = V("M", 0, [KC, TT], BF16, "mergedT")
        junk = V("M", 0, [D], BF16, "junk")
        xhalo = V("M", D * 2, [D], F32, "xhalo")
        hTh = V("U", 0, [KC, P], BF16, "hTh")
        u_ap = tens["U"][:, 0:G * TT].rearrange("p (g t) -> p g t", t=TT)
        u_b = [S.buf("U", g * TT * 2, (g + 1) * TT * 2, "u%d" % g) for g in range(G)]
        xP = [V("U", i * 2048, [512], F32, "xP%d" % i) for i in range(4)]
        ggP = [V("U", 8192 + i * 2048, [512], F32, "ggP%d" % i) for i in range(2)]
        junk2 = V("U", 12288, [512], BF16, "junk2")
        rep = V("U", 13312, [P], F32, "rep")
        vn_ap = tens["Y"][:, o_vnq // 2:o_vnq // 2 + NCH * SW].rearrange("p (c s) -> p c s", s=SW)
        vn_b = [S.buf("Y", o_vnq + ch * SW * 2, o_vnq + (ch + 1) * SW * 2, "vn%d" % ch) for ch in range(NCH)]
        qT_ap = tens["Y"][:, o_vnq // 2:o_vnq // 2 + NQ * TT].rearrange("p (h t) -> p h t", t=TT)
        qT_b = [S.buf("Y", o_vnq + h * TT * 2, o_vnq + (h + 1) * TT * 2, "q%d" % h) for h in range(NQ)]
        kT_ap = tens["Y"][:, o_k // 2:o_k // 2 + NKV * 640].rearrange("p (h t) -> p h t", t=640)
        kT_b = [S.buf("Y", o_k + h * 1280, o_k + (h + 1) * 1280, "k%d" % h) for h in range(NKV)]
        v_ap = tens["Y"][:, o_v // 2:o_v // 2 + 5 * KVW].rearrange("p (c w) -> p c w", w=KVW)
        v_b = S.buf("Y", o_v, o_v + 5 * KVW * 2, "v")
        Ccs = V("Y", o_cs, [640], F32, "Ccs", 32); Scs = V("Y", o_cs + 2560, [640], F32, "Scs", 32)
        sm2 = [V("Y", o_scr + i * 4096, [4, 256], F32, "sm%d" % i) for i in range(2)]
        pn2 = [V("Y", o_scr + 8192 + i * 2048, [4, 256], BF16, "pn%d" % i) for i in range(2)]
        pT = V("Y", o_scr + 12288, [8, P], BF16, "pT")
        gtmp = [V("Y", o_scr + i * 2048, [512], F32, "gtmp%d" % i) for i in range(2)]
        stmp = V("Y", o_scr + 4096, [512], F32, "stmp")
        ropef = V("Y", o_scr, [512], F32, "ropef", 32); ropet = V("Y", o_scr + 2048, [512], F32, "ropet", 32)
        ang = V("Y", o_scr + 4096, [640], F32, "ang", 32); angi = V("Y", o_scr + 8192, [640], I32, "angi", 32)
        sga = V("Y", o_scr, [2, 512], F32, "sga"); sgb = V("Y", o_scr + 4096, [2, 512], F32, "sgb")
        mt1 = V("Y", o_scr + 8192, [2, 512], F32, "mt1")
        oT_ap = tens["Y"][:, o_oT // 2:o_oT // 2 + NQ * TT].rearrange("p (h t) -> p h t", t=TT)
        oT_b = [S.buf("Y", o_oT + hk * 4 * TT * 2, o_oT + (hk + 1) * 4 * TT * 2, "oT%d" % hk) for hk in range(NKV)]
        actb = [V("M", i * 8 * TT * 2, [8, TT], BF16, "actb%d" % i) for i in range(2)]
        sgt = [V("U", i * 2048, [512], F32, "sgt%d" % i) for i in range(4)]

        SCALE = float(128 ** -0.5)
        evac_rr = [0]

        def fm_group(W, col0, ncols, Kc, rhs_fn, rhs_bufs, banks, N=TT, rhs2_fn=None, rhs2_bufs=(), banks2=None, N2=P, krow0=0):
            nk = min(Kc, SLOT_ELEMS // ncols)
            ncc = ncols // P
            for s0 in range(0, Kc, nk):
                nke = min(nk, Kc - s0)
                slab, slb = load_slab(W, krow0 + s0, nke, col0, ncols)

                def emit():
                    ins = None
                    for cc in range(ncc):
                        for kl in range(nke):
                            kc = s0 + kl
                            ins = pe.matmul(banks[cc].ap[:, 0:N], lhsT=slab[:, kl, cc * P:(cc + 1) * P], rhs=rhs_fn(kc),
                                            start=(kc == 0), stop=(kc == Kc - 1))
                        if rhs2_fn is not None:
                            for kl in range(nke):
                                kc = s0 + kl
                                ins = pe.matmul(banks2[cc].ap[:, 0:N2], lhsT=slab[:, kl, cc * P:(cc + 1) * P], rhs=rhs2_fn(kc),
                                                start=(kc == 0), stop=(kc == Kc - 1))
                    return ins
                wr = [b.b for b in banks[:ncc]] + ([b.b for b in banks2[:ncc]] if rhs2_fn is not None else [])
                S.op("pe", emit, [slb] + list(rhs_bufs) + list(rhs2_bufs), wr)

        def tm_group(W, krow0, Kc, col0, ncols, lhs_fn, lhs_bufs, banks, chunks=range(NCH), lhs2_fn=None, lhs2_bufs=(), bank2=None):
            nk = min(Kc, SLOT_ELEMS // ncols)
            for s0 in range(0, Kc, nk):
                nke = min(nk, Kc - s0)
                slab, slb = load_slab(W, krow0 + s0, nke, col0, ncols)

                for ch in chunks:
                    def emit():
                        ins = None
                        for kl in range(nke):
                            kc = s0 + kl
                            ins = pe.matmul(banks[ch].ap[:, 0:ncols], lhsT=lhs_fn(kc, ch), rhs=slab[:, kl, :],
                                            start=(kc == 0), stop=(kc == Kc - 1))
                        return ins
                    S.op("pe", emit, [slb] + list(lhs_bufs), [banks[ch].b])
                if lhs2_fn is not None:
                    def emit():
                        ins = None
                        for kl in range(nke):
                            kc = s0 + kl
                            ins = pe.matmul(bank2.ap[:, 0:ncols], lhsT=lhs2_fn(kc), rhs=slab[:, kl, :],
                                            start=(kc == 0), stop=(kc == Kc - 1))
                        return ins
                    S.op("pe", emit, [slb] + list(lhs2_bufs), [bank2.b])

        def build_h(src_ap, src_bufs, dst_ap, dst_buf, avec, bvec, sidx, tcol0, ncols_tok=P):
            sq = TV(ssq.ap[:, sidx:sidx + 1], ssq.b); rsd = TV(rstd.ap[:, sidx:sidx + 1], rstd.b)
            S.op("act", lambda: act.activation(out=junk.ap, in_=src_ap, func=AF.Square, accum_out=sq.ap),
                 list(src_bufs), [junk.b, sq.b])
            S.op("act", lambda: act.activation(out=rsd.ap, in_=sq.ap, func=AF.Sqrt, bias=epsR.ap, scale=1.0 / D), [sq.b, epsR.b], [rsd.b])
            S.op("dve", lambda: dve.reciprocal(out=rsd.ap, in_=rsd.ap), [rsd.b], [rsd.b])
            S.op("dve", lambda: dve.tensor_scalar(out=src_ap, in0=src_ap, scalar1=rsd.ap, scalar2=None, op0=ALU.mult),
                 list(src_bufs) + [rsd.b], list(src_bufs))
            for kg in range(KC // 4):
                pb = PS[kg % 2]

                def emit():
                    ins = None
                    for j in range(4):
                        kc = kg * 4 + j
                        ins = pe.transpose(out=pb.ap[:, j * P:(j + 1) * P], in_=src_ap[:, kc * P:(kc + 1) * P], identity=identF.ap)
                    return ins
                S.op("pe", emit, list(src_bufs) + [identF.b], [pb.b])
                for j in range(4):
                    kc = kg * 4 + j
                    o_ = dst_ap[:, kc, tcol0:tcol0 + P]
                    i_ = pb.ap[:, j * P:(j + 1) * P]
                    if kg % 2 == 0:
                        S.op("act", lambda: act.activation(out=o_, in_=i_, func=AF.Identity, bias=bvec.ap[:, kc:kc + 1],
                                                           scale=avec.ap[:, kc:kc + 1]), [pb.b, avec.b, bvec.b], [dst_buf])
                    else:
                        S.op("dve", lambda: dve.tensor_scalar(out=o_, in0=i_, scalar1=avec.ap[:, kc:kc + 1], scalar2=bvec.ap[:, kc:kc + 1],
                                                              op0=ALU.mult, op1=ALU.add), [pb.b, avec.b, bvec.b], [dst_buf])

        ssq_c = [S.buf("C", ssq.b.lo + 4 * i, ssq.b.lo + 4 * i + 4, "ssq%d" % i) for i in range(8)]
        rstd_c = [S.buf("C", rstd.b.lo + 4 * i, rstd.b.lo + 4 * i + 4, "rstd%d" % i) for i in range(8)]
        hTk = [S.buf("H", kc * TT * 2, (kc + 1) * TT * 2, "hT%d" % kc) for kc in range(KC)]

        def build_h_tile(avec, bvec):
            for ch in range(NCH):
                sqa = ssq.ap[:, ch:ch + 1]; rsa = rstd.ap[:, ch:ch + 1]
                S.op("act", lambda: act.activation(out=junk.ap, in_=Ych[ch], func=AF.Square, accum_out=sqa), Yb(ch), [junk.b, ssq_c[ch]])
                S.op("act", lambda: act.activation(out=rsa, in_=sqa, func=AF.Sqrt, bias=epsR.ap, scale=1.0 / D), [ssq_c[ch], epsR.b], [rstd_c[ch]])
                S.op("dve", lambda: dve.reciprocal(out=rsa, in_=rsa), [rstd_c[ch]], [rstd_c[ch]])
                S.op("dve", lambda: dve.tensor_scalar(out=Ych[ch], in0=Ych[ch], scalar1=rsa, scalar2=None, op0=ALU.mult),
                     Yb(ch) + [rstd_c[ch]], Yb(ch))
            allY = [b_ for ch in range(NCH) for b_ in Yb(ch)]
            for kc in range(KC):
                pb = PS[kc % 4]

                def emit():
                    ins = None
                    for ch in range(NCH):
                        ins = pe.transpose(out=pb.ap[:, ch * P:(ch + 1) * P], in_=Ych[ch][:, kc * P:(kc + 1) * P], identity=identF.ap)
                    return ins
                S.op("pe", emit, allY + [identF.b], [pb.b])
                if kc % 2 == 0:
                    S.op("act", lambda: act.activation(out=hT.ap[:, kc, :], in_=pb.ap, func=AF.Identity, bias=bvec.ap[:, kc:kc + 1],
                                                       scale=avec.ap[:, kc:kc + 1]), [pb.b, avec.b, bvec.b], [hTk[kc]])
                else:
                    S.op("dve", lambda: dve.tensor_scalar(out=hT.ap[:, kc, :], in0=pb.ap, scalar1=avec.ap[:, kc:kc + 1], scalar2=bvec.ap[:, kc:kc + 1],
                                                          op0=ALU.mult, op1=ALU.add), [pb.b, avec.b, bvec.b], [hTk[kc]])

        def rope_evac(bank, dstT_ap, dst_buf, col0, N, cs_col0, rbank):
            S.op("act", lambda: act.copy(out=dstT_ap[:, col0:col0 + N], in_=bank.ap[:, 0:N]), [bank.b], [dst_buf])
            S.op("pe", lambda: pe.matmul(rbank.ap[:, 0:N], lhsT=rotB.ap, rhs=dstT_ap[:, col0:col0 + N], start=True, stop=True),
                 [rotB.b, dst_buf], [rbank.b])
            S.op("dve", lambda: dve.tensor_tensor(out=ropet.ap[:, 0:N], in0=rbank.ap[0:32, 0:N], in1=Scs.ap[:, cs_col0:cs_col0 + N], op=ALU.mult),
                 [rbank.b, Scs.b], [ropet.b])
            S.op("dve", lambda: dve.tensor_tensor(out=ropef.ap[:, 0:N], in0=bank.ap[0:32, 0:N], in1=Ccs.ap[:, cs_col0:cs_col0 + N], op=ALU.mult),
                 [bank.b, Ccs.b], [ropef.b])
            S.op("dve", lambda: dve.tensor_tensor(out=dstT_ap[0:32, col0:col0 + N], in0=ropef.ap[:, 0:N], in1=ropet.ap[:, 0:N], op=ALU.add),
                 [ropef.b, ropet.b], [dst_buf])

        def gen_ggP(ggvec, cg, dst):
            pb = PS[6 + (cg % 2)]
            for jj in range(4):
                col = cg * 4 + jj
                S.op("dve", lambda: dve.tensor_scalar(out=rep.ap, in0=onesF.ap, scalar1=ggvec.ap[:, col:col + 1], scalar2=None, op0=ALU.mult),
                     [onesF.b, ggvec.b], [rep.b])
                S.op("pe", lambda: pe.matmul(pb.ap[:, jj * P:(jj + 1) * P], lhsT=rep.ap, rhs=identF.ap, start=True, stop=True),
                     [rep.b, identF.b], [pb.b])
            S.op("act", lambda: act.copy(out=dst.ap, in_=pb.ap), [pb.b], [dst.b])

        def combine(ti, gv, src_is_x):
            S.op("dve", lambda: dve.tensor_reduce(out=ssq.ap[:, 0:NCH], in_=ssqp.ap, axis=AX.X, op=ALU.add), [ssqp.b], [ssq.b])
            S.op("act", lambda: act.activation(out=rstd.ap[:, 0:NCH], in_=ssq.ap[:, 0:NCH], func=AF.Sqrt, bias=epsR.ap, scale=1.0 / D),
                 [ssq.b, epsR.b], [rstd.b])
            S.op("dve", lambda: dve.reciprocal(out=rstd.ap[:, 0:NCH], in_=rstd.ap[:, 0:NCH]), [rstd.b], [rstd.b])
            ckpt(132)
            items = [(cg, ch) for cg in range(NCG) for ch in range(NCH)]

            def issue_load(k):
                cg, ch = items[k]
                xp = xP[k % 4]
                r0 = ti * TT + ch * P
                src = (x_d if src_is_x else out_d)[r0:r0 + P, cg * 512:(cg + 1) * 512]
                S.dma("sp", ("xp", k % 4), lambda: nc.sync.dma_start(out=xp.ap, in_=src),
                      reads=([] if src_is_x else [out_b[ti][ch][cg]]), writes=[xp.b])
            def issue_gg(cg):
                gp_ = ggP[cg % 2]
                S.dma("sp", ("ggld", cg % 2), lambda: nc.sync.dma_start(out=gp_.ap, in_=ggd_d[gv * P:(gv + 1) * P, cg * 512:(cg + 1) * 512]),
                      reads=[ggd_b[gv][cg]], writes=[gp_.b])
            issue_gg(0)
            if NCG > 1:
                issue_gg(1)
            for k in range(min(3, len(items))):
                issue_load(k)
            for k, (cg, ch) in enumerate(items):
                gp = ggP[cg % 2]
                if ch == 0 and cg >= 1 and cg + 1 < NCG:
                    issue_gg(cg + 1)
                xp = xP[k % 4]
                r0 = ti * TT + ch * P
                yb = Ycc[ch][cg]
                ob = out_b[ti][ch][cg]
                S.op("dve", lambda: dve.scalar_tensor_tensor(out=yb.ap, in0=yb.ap, scalar=rstd.ap[:, ch:ch + 1], in1=gp.ap,
                                                             op0=ALU.mult, op1=ALU.mult), [yb.b, rstd.b, gp.b], [yb.b])
                S.op("dve", lambda: dve.tensor_tensor(out=yb.ap, in0=yb.ap, in1=xp.ap, op=ALU.add), [yb.b, xp.b], [yb.b])
                S.dma("sp", ("st", ch, cg), lambda: nc.sync.dma_start(out=out_d[r0:r0 + P, cg * 512:(cg + 1) * 512], in_=yb.ap),
                      reads=[yb.b], writes=[ob])
                ckpt(134)
                if k + 3 < len(items):
                    issue_load(k + 3)

        ggd_b = [[S.buf("GGD", v * NCG + cg, v * NCG + cg + 1) for cg in range(NCG)] for v in range(2)]
        for v, ggvec in enumerate((gg1, gg2)):
            for cg in range(NCG):
                gp = ggP[cg % 2]
                gen_ggP(ggvec, cg, gp)
                S.dma("sp", ("ggst", cg % 2), lambda: nc.sync.dma_start(out=ggd_d[v * P:(v + 1) * P, cg * 512:(cg + 1) * 512], in_=gp.ap),
                      reads=[gp.b], writes=[ggd_b[v][cg]])

        for ti in range(NT):
            t0 = ti * TT
            if ti == 0:
                S.dma("sp", "xh", lambda: nc.sync.dma_start(out=xhalo.ap, in_=xh_d[:, :]), reads=[], writes=[xhalo.b])
                build_h(xhalo.ap, [xhalo.b], hTh.ap, hTh.b, a1, sh1, 4, 0)
            for ch in range(NCH):
                r0 = t0 + ch * P
                S.dma("sp", ("xl", ch), lambda: nc.sync.dma_start(out=Ych[ch], in_=x_d[r0:r0 + P, :]), reads=[], writes=Yb(ch))
            build_h_tile(a1, sh1)

            if ti == 0: ckpt(4)
            S.dma("sp", "pos", lambda: nc.sync.dma_start(out=angi.ap, in_=pos_d[:, t0:t0 + 640]),
                  reads=[], writes=[angi.b])
            S.op("dve", lambda: dve.tensor_copy(out=ang.ap, in_=angi.ap), [angi.b], [ang.b])
            S.op("dve", lambda: dve.tensor_scalar(out=ang.ap, in0=ang.ap, scalar1=invf.ap, scalar2=None, op0=ALU.mult), [ang.b, invf.b], [ang.b])
            S.op("dve", lambda: dve.tensor_scalar(out=angi.ap, in0=ang.ap, scalar1=float(1.0 / (2 * np.pi)), scalar2=None, op0=ALU.mult), [ang.b], [angi.b])
            S.op("dve", lambda: dve.tensor_copy(out=Scs.ap, in_=angi.ap), [angi.b], [Scs.b])
            S.op("dve", lambda: dve.scalar_tensor_tensor(out=ang.ap, in0=Scs.ap, scalar=float(-2 * np.pi), in1=ang.ap, op0=ALU.mult, op1=ALU.add),
                 [Scs.b, ang.b], [ang.b])
            S.op("act", lambda: act.activation(out=Scs.ap, in_=ang.ap, func=AF.Sin, scale=0.5), [ang.b], [Scs.b])
            S.op("act", lambda: act.activation(out=Ccs.ap, in_=ang.ap, func=AF.Sin, scale=0.25), [ang.b], [Ccs.b])
            S.op("dve", lambda: dve.tensor_tensor(out=Ccs.ap, in0=Ccs.ap, in1=Ccs.ap, op=ALU.mult), [Ccs.b], [Ccs.b])
            S.op("dve", lambda: dve.tensor_scalar(out=Ccs.ap, in0=Ccs.ap, scalar1=-2.0, scalar2=1.0, op0=ALU.mult, op1=ALU.add), [Ccs.b], [Ccs.b])
            S.op("dve", lambda: dve.tensor_tensor(out=ang.ap, in0=Scs.ap, in1=Scs.ap, op=ALU.mult), [Scs.b], [ang.b])
            S.op("dve", lambda: dve.scalar_tensor_tensor(out=Scs.ap, in0=Scs.ap, scalar=2.0, in1=Ccs.ap, op0=ALU.mult, op1=ALU.mult),
                 [Scs.b, Ccs.b], [Scs.b])
            S.op("dve", lambda: dve.tensor_scalar(out=Ccs.ap, in0=ang.ap, scalar1=-2.0, scalar2=1.0, op0=ALU.mult, op1=ALU.add), [ang.b], [Ccs.b])

            if ti == 0: ckpt(5)
            hT_rhs = lambda kc: hT.ap[:, kc, :]
            if ti > 0:
                for hk in range(NKV):
                    S.op("dve", lambda: dve.tensor_copy(out=kT_ap[:, hk, 0:P], in_=khalo.ap[:, hk, :]), [khalo.b], [kT_b[hk]])
                S.op("dve", lambda: dve.tensor_copy(out=v_ap[:, 0, :], in_=vhalo.ap), [vhalo.b], [v_b])
            kstep = 2 if NKV >= 2 else 1
            for hk0 in range(0, NKV, kstep):
                banks = [PS[2 + i] for i in range(kstep)]; banks2 = [PS[4 + i] for i in range(kstep)]
                fm_group(win_d, c.ok + hk0 * P, kstep * P, KC, hT_rhs, [hT.b], banks,
                         rhs2_fn=(lambda kc: hTh.ap[:, kc, :]) if ti == 0 else None, rhs2_bufs=[hTh.b] if ti == 0 else (), banks2=banks2)
                if ti == 0: ckpt(51)
                for i in range(kstep):
                    hk = hk0 + i
                    rope_evac(banks[i], kT_ap[:, hk], kT_b[hk], P, TT, P, PS[6])
                    if ti == 0: ckpt(52)
                    if ti == 0:
                        rope_evac(banks2[i], kT_ap[:, hk], kT_b[hk], 0, P, 0, PS[7])
            if ti == 0: ckpt(6)
            vb = [PS[ch] for ch in range(NCH)]
            tm_group(win_d, 0, KC, c.ov, KVW, lambda kc, ch: hT.ap[:, kc, ch * P:(ch + 1) * P], [hT.b], vb,
                     lhs2_fn=(lambda kc: hTh.ap[:, kc, :]) if ti == 0 else None, lhs2_bufs=[hTh.b] if ti == 0 else (), bank2=PS[4])
            for ch in range(NCH):
                S.op("act" if ch % 2 == 0 else "dve",
                     (lambda: act.copy(out=v_ap[:, 1 + ch, :], in_=vb[ch].ap[:, 0:KVW])) if ch % 2 == 0 else
                     (lambda: dve.tensor_copy(out=v_ap[:, 1 + ch, :], in_=vb[ch].ap[:, 0:KVW])), [vb[ch].b], [v_b])
            if ti == 0:
                S.op("act", lambda: act.copy(out=v_ap[:, 0, :], in_=PS[4].ap[:, 0:KVW]), [PS[4].b], [v_b])
            if ti + 1 < NT:
                for hk in range(NKV):
                    S.op("dve", lambda: dve.tensor_copy(out=khalo.ap[:, hk, :], in_=kT_ap[:, hk, TT:TT + P]), [kT_b[hk]], [khalo.b])
                S.op("dve", lambda: dve.tensor_copy(out=vhalo.ap, in_=v_ap[:, 4, :]), [v_b], [vhalo.b])

            if ti == 0: ckpt(7)
            for sg in range(c.NSG):
                bk = [PS[4 * (sg % 2) + ch] for ch in range(NCH)]
                tm_group(win_d, 0, KC, c.osv + sg * c.SGW, c.SGW, lambda kc, ch: hT.ap[:, kc, ch * P:(ch + 1) * P], [hT.b], bk)
                for ch in range(NCH):
                    gt = gtmp[ch % 2]
                    S.op("act", lambda: act.activation(out=gt.ap[:, 0:c.SGW], in_=bk[ch].ap[:, 0:c.SGW], func=AF.Gelu), [bk[ch].b], [gt.b])
                    S.op("dve", lambda: dve.bn_stats(out=stats.ap[:, ch, sg * 6:(sg + 1) * 6], in_=gt.ap[:, 0:c.SGW]), [gt.b], [stats.b])
                    S.op("dve", lambda: dve.tensor_copy(out=vn_ap[:, ch, sg * c.SGW:(sg + 1) * c.SGW], in_=gt.ap[:, 0:c.SGW]), [gt.b], [vn_b[ch]])
            for ch in range(NCH):
                S.op("dve", lambda: dve.bn_aggr(out=mv.ap[:, ch, :], in_=stats.ap[:, ch, :]), [stats.b], [mv.b])
            S.op("act", lambda: act.activation(out=lnr.ap, in_=mv.ap[:, :, 1], func=AF.Sqrt, bias=epsL.ap, scale=1.0), [mv.b, epsL.b], [lnr.b])
            S.op("dve", lambda: dve.reciprocal(out=lnr.ap, in_=lnr.ap), [lnr.b], [lnr.b])
            S.op("dve", lambda: dve.scalar_tensor_tensor(out=lnm.ap, in0=mv.ap[:, :, 0], scalar=-1.0, in1=lnr.ap, op0=ALU.mult, op1=ALU.mult),
                 [mv.b, lnr.b], [lnm.b])
            for ch in range(NCH):
                S.op("dve", lambda: dve.tensor_scalar(out=vn_ap[:, ch, :], in0=vn_ap[:, ch, :], scalar1=lnr.ap[:, ch:ch + 1], scalar2=lnm.ap[:, ch:ch + 1],
                                                      op0=ALU.mult, op1=ALU.add), [vn_b[ch], lnr.b, lnm.b], [vn_b[ch]])

            if ti == 0: ckpt(8)
            for gp_ in range(G // 2):
                banks = [PS[2 * (gp_ % 4)], PS[2 * (gp_ % 4) + 1]]
                fm_group(win_d, c.osu + gp_ * 256, 256, KC, hT_rhs, [hT.b], banks)
                for i in range(2):
                    g = gp_ * 2 + i
                    S.op("act", lambda: act.activation(out=u_ap[:, g, :], in_=banks[i].ap, func=AF.Gelu), [banks[i].b], [u_b[g]])

            if ti == 0: ckpt(9)
            for g in range(G):
                pb = PS[g % 4]

                def emit():
                    ins = None
                    for ch in range(NCH):
                        ins = pe.matmul(pb.ap[:, ch * P:(ch + 1) * P], lhsT=vn_ap[:, ch, g * P:(g + 1) * P], rhs=WmT.ap[:, g, :], start=True, stop=True)
                    return ins
                S.op("pe", emit, vn_b + [WmT.b], [pb.b])
                S.op("dve", lambda: dve.scalar_tensor_tensor(out=stmp.ap.rearrange("p (c t) -> p c t", t=P), in0=pb.ap.rearrange("p (c t) -> p c t", t=P),
                                                             scalar=lng.ap[:, g:g + 1], in1=bias2.ap[:, g, :].unsqueeze(1).broadcast_to([P, NCH, P]),
                                                             op0=ALU.mult, op1=ALU.add), [pb.b, lng.b, bias2.b], [stmp.b])
                S.op("dve", lambda: dve.tensor_tensor(out=u_ap[:, g, :], in0=stmp.ap, in1=u_ap[:, g, :], op=ALU.mult), [stmp.b, u_b[g]], [u_b[g]])

            if ti == 0: ckpt(10)
            for hp in range(NQ // 2):
                banks = [PS[2 * (hp % 3)], PS[2 * (hp % 3) + 1]]
                fm_group(win_d, c.oq + hp * 256, 256, KC, hT_rhs, [hT.b], banks)
                for i in range(2):
                    h = hp * 2 + i
                    rope_evac(banks[i], qT_ap[:, h], qT_b[h], 0, TT, P, PS[6 + i])

            if ti == 0: ckpt(11)
            groups = [(b_, hk_) for b_ in range(NCH) for hk_ in range(NKV)]

            def att_banks(gi):
                par = gi % 2
                return [PS[par * 3 + 0], PS[par * 3 + 1]], PS[par * 3 + 2], PS[6]

            def att_scores(gi):
                b_, hk_ = groups[gi]
                ps_s, _, _ = att_banks(gi)

                def emit():
                    ins = None
                    for hh in range(4):
                        h = hk_ * 4 + hh
                        ins = pe.matmul(ps_s[hh // 2].ap[:, (hh % 2) * 256:(hh % 2) * 256 + 256], lhsT=qT_ap[:, h, b_ * P:(b_ + 1) * P],
                                        rhs=kT_ap[:, hk_, b_ * P:b_ * P + 256], start=True, stop=True)
                    return ins
                S.op("pe", emit, [qT_b[hk_ * 4 + hh] for hh in range(4)] + [kT_b[hk_]], [ps_s[0].b, ps_s[1].b])

            att_scores(0)
            for gi, (b, hk) in enumerate(groups):
                ps_s, ps_t, ps_o = att_banks(gi)
                smx = sm2[gi % 2]; pnx = pn2[gi % 2]
                msk = mask0 if (ti == 0 and b == 0) else maskB
                for h2 in range(2):
                    S.op("dve", lambda: dve.tensor_tensor(out=smx.ap[:, 2 * h2:2 * h2 + 2, :], in0=ps_s[h2].ap.rearrange("p (a k) -> p a k", k=256),
                                                          in1=msk.ap.unsqueeze(1).broadcast_to([P, 2, 256]), op=ALU.add), [ps_s[h2].b, msk.b], [smx.b])
                S.op("dve", lambda: dve.tensor_reduce(out=mx.ap[:, 0:4], in_=smx.ap, axis=AX.X, op=ALU.max), [smx.b], [mx.b])
                S.op("dve", lambda: dve.scalar_tensor_tensor(out=negm.ap[:, 0:4], in0=mx.ap[:, 0:4], scalar=-SCALE, in1=nsink.ap[:, hk * 4:hk * 4 + 4],
                                                             op0=ALU.mult, op1=ALU.min), [mx.b, nsink.b], [negm.b])
                S.op("dve", lambda: dve.tensor_tensor(out=esk.ap[:, 0:4], in0=negm.ap[:, 0:4], in1=sinkB.ap[:, hk * 4:hk * 4 + 4], op=ALU.add),
                     [negm.b, sinkB.b], [esk.b])
                if gi + 1 < len(groups):
                    att_scores(gi + 1)
                for hh in range(4):
                    h = hk * 4 + hh
                    S.op("act", lambda: act.activation(out=smx.ap[:, hh, :], in_=smx.ap[:, hh, :], func=AF.Exp, bias=negm.ap[:, hh:hh + 1], scale=SCALE,
                                                       accum_out=rs.ap[:, hh:hh + 1]), [smx.b, negm.b], [smx.b, rs.b])
                S.op("act", lambda: act.activation(out=esk.ap[:, 0:4], in_=esk.ap[:, 0:4], func=AF.Exp), [esk.b], [esk.b])
                S.op("dve", lambda: dve.tensor_tensor(out=den.ap[:, 0:4], in0=rs.ap[:, 0:4], in1=esk.ap[:, 0:4], op=ALU.add), [rs.b, esk.b], [den.b])
                S.op("dve", lambda: dve.reciprocal(out=den.ap[:, 0:4], in_=den.ap[:, 0:4]), [den.b], [den.b])
                for hh in range(4):
                    S.op("dve", lambda: dve.tensor_scalar(out=pnx.ap[:, hh, :], in0=smx.ap[:, hh, :], scalar1=den.ap[:, hh:hh + 1], scalar2=None, op0=ALU.mult),
                         [smx.b, den.b], [pnx.b])
                pst = ps_t.ap.bitcast(BF16)

                def emit():
                    ins = None
                    for hh in range(4):
                        for k2 in range(2):
                            ins = pe.transpose(out=pst[:, (hh * 2 + k2) * P:(hh * 2 + k2 + 1) * P], in_=pnx.ap[:, hh, k2 * P:(k2 + 1) * P], identity=identB.ap)
                    return ins
                S.op("pe", emit, [pnx.b, identB.b], [ps_t.b])
                S.op("act", lambda: act.copy(out=pT.ap.rearrange("p a b -> p (a b)"), in_=pst), [ps_t.b], [pT.b])

                def emit():
                    ins = None
                    for hh in range(4):
                        for k2 in range(2):
                            ins = pe.matmul(ps_o.ap[:, hh * P:(hh + 1) * P], lhsT=v_ap[:, b + k2, hk * P:(hk + 1) * P], rhs=pT.ap[:, hh * 2 + k2, :],
                                            start=(k2 == 0), stop=(k2 == 1))
                    return ins
                S.op("pe", emit, [v_b, pT.b], [ps_o.b])
                S.op("act", lambda: act.copy(out=oT_ap[:, hk * 4:hk * 4 + 4, b * P:(b + 1) * P], in_=ps_o.ap.rearrange("p (h q) -> p h q", q=P)),
                     [ps_o.b], [oT_b[hk]])

            if ti == 0: ckpt(12)
            for j2 in range(KC // 2):
                col0 = j2 * 256
                bA = [PS[0], PS[1]]; bB = [PS[2], PS[3]]; bS = [PS[4], PS[5]]; bT = [PS[6], PS[7]]
                fm_group(win_d, c.oga + col0, 256, KC, hT_rhs, [hT.b], bA)
                S.op("act", lambda: act.activation(out=sga.ap[:, 0, :], in_=bA[0].ap, func=AF.Sigmoid), [bA[0].b], [sga.b])
                S.op("act", lambda: act.activation(out=sga.ap[:, 1, :], in_=bA[1].ap, func=AF.Sigmoid), [bA[1].b], [sga.b])
                fm_group(win_d, c.ogb + col0, 256, KC, hT_rhs, [hT.b], bB)
                S.op("act", lambda: act.activation(out=sgb.ap[:, 0, :], in_=bB[0].ap, func=AF.Sigmoid), [bB[0].b], [sgb.b])
                S.op("act", lambda: act.activation(out=sgb.ap[:, 1, :], in_=bB[1].ap, func=AF.Sigmoid), [bB[1].b], [sgb.b])
                fm_group(wps_d, col0, 256, G, lambda kc: u_ap[:, kc, :], u_b, bS)
                for i in range(2):
                    S.op("dve", lambda: dve.tensor_tensor(out=mt1.ap[:, i, :], in0=bS[i].ap, in1=sga.ap[:, i, :], op=ALU.mult), [bS[i].b, sga.b], [mt1.b])
                fm_group(wpa_d, col0, 256, NQ, lambda kc: oT_ap[:, kc, :], oT_b, bT)
                for i in range(2):
                    S.op("dve", lambda: dve.tensor_tensor(out=sgb.ap[:, i, :], in0=bT[i].ap, in1=sgb.ap[:, i, :], op=ALU.mult), [bT[i].b, sgb.b], [sgb.b])
                S.op("dve", lambda: dve.tensor_tensor(out=mergedT.ap[:, j2 * 2:j2 * 2 + 2, :], in0=mt1.ap, in1=sgb.ap, op=ALU.add), [mt1.b, sgb.b], [mergedT.b])

            if ti == 0: ckpt(13)
            for cg in range(NCG):
                bk = [PS[4 * (cg % 2) + ch] for ch in range(NCH)]
                tm_group(wout_d, 0, KC, cg * 512, 512, lambda kc, ch: mergedT.ap[:, kc, ch * P:(ch + 1) * P], [mergedT.b], bk)
                for ch in range(NCH):
                    S.op("dve", lambda: dve.tensor_copy(out=Ycc[ch][cg].ap, in_=bk[ch].ap), [bk[ch].b], [Ycc[ch][cg].b])
                    S.op("act", lambda: act.activation(out=junk2.ap, in_=Ycc[ch][cg].ap, func=AF.Square, accum_out=ssqp.ap[:, ch, cg:cg + 1]),
                         [Ycc[ch][cg].b], [junk2.b, ssqp.b])
            if ti == 0: ckpt(131)
            combine(ti, 0, True)

            if ti == 0: ckpt(14)
            build_h_tile(a2, sh2)

            if ti == 0: ckpt(15)
            NHB = (HC + 7) // 8
            for hb in range(NHB):
                nch_ = min(8, HC - hb * 8)
                ab = actb[hb % 2]
                q0 = 0
                while q0 < nch_:
                    w_ = 4 if nch_ - q0 >= 4 else 2
                    col0 = (hb * 8 + q0) * P
                    bG = [PS[i] for i in range(w_)]; bU = [PS[4 + i] for i in range(w_)]
                    fm_group(wg_d, col0, w_ * P, KC, hT_rhs, [hT.b], bG)
                    for i in range(w_):
                        S.op("act", lambda: act.activation(out=sgt[i].ap, in_=bG[i].ap, func=AF.Silu), [bG[i].b], [sgt[i].b])
                    fm_group(wu_d, col0, w_ * P, KC, hT_rhs, [hT.b], bU)
                    for i in range(w_):
                        S.op("dve", lambda: dve.tensor_tensor(out=ab.ap[:, q0 + i, :], in0=sgt[i].ap, in1=bU[i].ap, op=ALU.mult),
                             [sgt[i].b, bU[i].b], [ab.b])
                    q0 += w_
                for cg in range(NCG):
                    bk = [PS[4 * (cg % 2) + ch] for ch in range(NCH)]
                    tm_group(wd_d, hb * 8, nch_, cg * 512, 512, lambda kc, ch: ab.ap[:, kc, ch * P:(ch + 1) * P], [ab.b], bk)
                    for ch in range(NCH):
                        yb = Ycc[ch][cg]
                        if hb == 0:
                            S.op("dve", lambda: dve.tensor_copy(out=yb.ap, in_=bk[ch].ap), [bk[ch].b], [yb.b])
                        else:
                            S.op("dve", lambda: dve.tensor_tensor(out=yb.ap, in0=bk[ch].ap, in1=yb.ap, op=ALU.add), [bk[ch].b, yb.b], [yb.b])
            for cg in range(NCG):
                for ch in range(NCH):
                    S.op("act", lambda: act.activation(out=junk2.ap, in_=Ycc[ch][cg].ap, func=AF.Square, accum_out=ssqp.ap[:, ch, cg:cg + 1]),
                         [Ycc[ch][cg].b], [junk2.b, ssqp.b])
            combine(ti, 1, False)

    except _Stop:
        S.dma('sp', ('st', 0, 0), lambda: nc.sync.dma_start(out=out_d[0:P, 0:512], in_=tens['C'][:, 0:1024].bitcast(F32)), reads=[], writes=[out_b[0][0][0]])
    allout = [out_b[ti][ch][cg] for ti in range(NT) for ch in range(NCH) for cg in range(NCG)]
    S.final_wait("sp", allout)
    es.close()
    return nc


def _consts():
    identF = np.eye(P, dtype=np.float32)
    rotT = np.zeros((P, P), np.float32)
    for m in range(16):
        rotT[m + 16, m] = -1.0
        rotT[m, m + 16] = 1.0
    qi = np.arange(P)[:, None]; ki = np.arange(256)[None, :]
    diff = qi + P - ki
    band = np.where((diff >= 0) & (diff < P), 0.0, NEG).astype(np.float32)
    s_ = np.arange(P)[:, None]; t_ = np.arange(P)[None, :]
    tri = (s_ <= t_).astype(np.float32)
    inv = (np.float32(ROPE_THETA) ** (-np.arange(0, 32, 2, dtype=np.float32) / np.float32(32))).astype(np.float32)
    invf = np.concatenate([inv, inv]).reshape(32, 1).astype(np.float32)
    return identF, rotT, band, tri, invf


def make_in_maps(cfg, inp):
    c = cfg
    f32 = np.float32
    identF, rotT, band, tri, invf = _consts()
    x = np.asarray(inp["x"], f32); cc = np.asarray(inp["c"], f32); pos = np.asarray(inp["positions"]).astype(np.int32)
    fm = lambda v, n: np.ascontiguousarray(np.asarray(v, f32).reshape(n, P).T)
    L = 0
    shared = {
        "w_ada": np.ascontiguousarray(np.asarray(inp["w_ada"], f32)[L]),
        "b_adaT": fm(np.asarray(inp["b_ada"])[L], 6 * c.KC),
        "g_pre_mixT": fm(np.asarray(inp["g_pre_mix"])[L], c.KC), "g_post_mixT": fm(np.asarray(inp["g_post_mix"])[L], c.KC),
        "g_pre_ffnT": fm(np.asarray(inp["g_pre_ffn"])[L], c.KC), "g_post_ffnT": fm(np.asarray(inp["g_post_ffn"])[L], c.KC),
        "w_in": np.ascontiguousarray(np.asarray(inp["w_in"], f32)[L]),
        "sinkB": np.ascontiguousarray(np.broadcast_to(np.asarray(inp["attn_sinks"], f32)[L][None, :], (P, c.NQ))),
        "ln_gT": fm(np.asarray(inp["sgu_ln_g"])[L], c.G), "ln_bT": fm(np.asarray(inp["sgu_ln_b"])[L], c.G),
        "sgu_w": np.ascontiguousarray(np.asarray(inp["sgu_w"], f32)[L]),
        "sgu_bB": np.ascontiguousarray(np.broadcast_to(np.asarray(inp["sgu_b"], f32)[L].reshape(1, -1), (P, c.G * P))),
        "w_proj_sgu": np.ascontiguousarray(np.asarray(inp["w_proj_sgu"], f32)[L]),
        "w_proj_attn": np.ascontiguousarray(np.asarray(inp["w_proj_attn"], f32)[L]),
        "w_out": np.ascontiguousarray(np.asarray(inp["w_out"], f32)[L]),
        "w_gate": np.ascontiguousarray(np.asarray(inp["w_gate"], f32)[L]),
        "w_up": np.ascontiguousarray(np.asarray(inp["w_up"], f32)[L]),
        "w_down": np.ascontiguousarray(np.asarray(inp["w_down"], f32)[L]),
        "identF": identF, "rotT": rotT, "maskB": band, "tri": tri, "invf": invf,
    }
    maps = []
    for i in range(c.NCORES):
        b = i // c.CPB; half = i % c.CPB; t0 = half * c.TOK
        m = dict(shared)
        m["x"] = np.ascontiguousarray(x[b, t0:t0 + c.TOK])
        if half > 0:
            m["xh"] = np.ascontiguousarray(x[b, t0 - P:t0])
            ph = pos[b, t0 - P:t0]
            m["mask0"] = band
        else:
            m["xh"] = np.zeros((P, c.D), f32)
            ph = np.zeros((P,), np.int32)
            mk = band.copy(); mk[:, 0:P] = NEG
            m["mask0"] = mk
        m["pos"] = np.ascontiguousarray(np.broadcast_to(np.concatenate([ph, pos[b, t0:t0 + c.TOK]]).reshape(1, -1).astype(np.int32), (32, c.TOK + P)))
        m["cT"] = fm(cc[b], c.KC)
        maps.append(m)
    return maps


def run(cfg, inp, trace=False, stop=None):
    nc = build_program(cfg, stop)
    maps = make_in_maps(cfg, inp)
    res = run_bass_kernel_spmd(nc, maps, core_ids=list(range(cfg.NCORES)), trace=trace)
    outs = [np.asarray(r["out"]) for r in res.results]
    B = cfg.BATCH
    full = np.stack([np.concatenate(outs[b * cfg.CPB:(b + 1) * cfg.CPB], axis=0) for b in range(B)], axis=0)
    return full.astype(np.float32), res


def kernel(**inputs):
    cfg = Cfg(4096, 4096, 4)
    out, _ = run(cfg, inputs)
    return out
```

```python
import numpy as np
from contextlib import ExitStack
import concourse.bass as bass
import concourse.mybir as mybir
from concourse.bass_utils import run_bass_kernel_spmd

F32 = mybir.dt.float32
BF16 = mybir.dt.bfloat16
I32 = mybir.dt.int32
AF = mybir.ActivationFunctionType
ALU = mybir.AluOpType
AX = mybir.AxisListType

P = 128
TT = 512
NCH = 4
RMS_EPS = 1e-6
LN_EPS = 1e-5
ROPE_THETA = 500000.0
NEG = -30000.0
SLOT_ELEMS = 4096
NSLOT = 4


class Cfg:
    def __init__(s, D, SEQ, BATCH, NCORES=8):
        s.D = D; s.KC = D // 128; s.NQ = D // 256; s.NKV = s.NQ // 4; s.G = D // 256
        s.SW = s.G * 128; s.AW = s.NQ * 128; s.KVW = s.NKV * 128
        s.HID = -(-(8 * D) // (3 * 256)) * 256; s.HC = s.HID // 128
        s.INW = s.AW + 2 * s.KVW + 2 * s.SW + 2 * D
        s.oq = 0; s.ok = s.AW; s.ov = s.ok + s.KVW; s.osu = s.ov + s.KVW
        s.osv = s.osu + s.SW; s.oga = s.osv + s.SW; s.ogb = s.oga + D
        s.SEQ = SEQ; s.BATCH = BATCH; s.NCORES = NCORES
        s.CPB = NCORES // BATCH
        s.TOK = SEQ // s.CPB; s.NT = s.TOK // TT
        s.NCG = D // 512
        s.NSG = max(1, s.SW // 512)
        s.SGW = min(512, s.SW)


class Buf:
    __slots__ = ("region", "lo", "hi", "last_w", "readers", "name")

    def __init__(self, region, lo, hi, name=""):
        self.region = region; self.lo = lo; self.hi = hi
        self.last_w = None
        self.readers = {}
        self.name = name


class Sched:
    def __init__(self, nc, es):
        self.nc = nc; self.es = es
        self.regions = {}
        self.eng = {}
        for name, h in (("pe", nc.tensor), ("act", nc.scalar), ("dve", nc.vector),
                        ("pool", nc.gpsimd), ("sp", nc.sync)):
            sem = es.enter_context(nc.semaphore("s_" + name))
            self.eng[name] = {"h": h, "sem": sem, "cnt": 0, "seen": {}}
        self.dsems = {}
        self.nwaits = 0

    def buf(self, region, lo, hi, name=""):
        b = Buf(region, lo, hi, name)
        self.regions.setdefault(region, []).append(b)
        return b

    def _conf(self, b):
        return [o for o in self.regions[b.region] if o.lo < b.hi and b.lo < o.hi]

    def _collect(self, engkey, reads, writes, is_dma):
        deps = {}

        def add(tok, kind):
            sem, val, ek = tok
            if (not is_dma) and ek == engkey and engkey == "pe" and kind != "raw":
                return
            k = id(sem)
            if k not in deps or deps[k][1] < val:
                deps[k] = (sem, val)
        for b in reads:
            for o in self._conf(b):
                if o.last_w is not None:
                    add(o.last_w, "raw")
                if b.region == "PS":
                    for ek, (sem, val) in o.readers.items():
                        if ek != engkey:
                            add((sem, val, ek), "rar")
        for b in writes:
            for o in self._conf(b):
                if o.last_w is not None:
                    add(o.last_w, "waw")
                for ek, (sem, val) in o.readers.items():
                    add((sem, val, ek), "war")
        return deps

    def _emit_waits(self, qname, deps):
        e = self.eng[qname]
        for k, (sem, val) in deps.items():
            if e["seen"].get(k, 0) >= val:
                continue
            e["h"].wait_ge(sem, val)
            e["seen"][k] = val
            self.nwaits += 1

    def _update(self, tok, reads, writes):
        sem, val, ek = tok
        for b in reads:
            b.readers[ek] = (sem, val)
        for b in writes:
            for o in self._conf(b):
                if o is not b:
                    o.last_w = None
                    o.readers = {}
            b.last_w = tok
            b.readers = {}

    def op(self, ename, emit, reads=(), writes=()):
        e = self.eng[ename]
        deps = self._collect(ename, reads, writes, False)
        self._emit_waits(ename, deps)
        ins = emit()
        e["cnt"] += 1
        ins.then_inc(e["sem"], 1)
        self._update((e["sem"], e["cnt"], ename), reads, writes)

    def dma(self, qname, semkey, emit, reads=(), writes=()):
        if semkey not in self.dsems:
            sem = self.es.enter_context(self.nc.semaphore("d_%d" % len(self.dsems)))
            self.dsems[semkey] = [sem, 0]
        ds = self.dsems[semkey]
        ek = ("dma", semkey)
        deps = self._collect(ek, reads, writes, True)
        self._emit_waits(qname, deps)
        ins = emit()
        ds[1] += 16
        ins.then_inc(ds[0], 16)
        self._update((ds[0], ds[1], ek), reads, writes)

    def final_wait(self, qname, bufs):
        deps = {}
        for b in bufs:
            for o in self._conf(b):
                if o.last_w is not None:
                    sem, val, ek = o.last_w
                    k = id(sem)
                    if k not in deps or deps[k][1] < val:
                        deps[k] = (sem, val)
        self._emit_waits(qname, deps)


class TV:
    __slots__ = ("ap", "b")

    def __init__(self, ap, b):
        self.ap = ap; self.b = b


class _Stop(Exception):
    pass


def build_program(cfg, stop=None):
    c = cfg
    D, KC, NQ, NKV, G, HC, TOK, NT, NCG = c.D, c.KC, c.NQ, c.NKV, c.G, c.HC, c.TOK, c.NT, c.NCG
    SW, AW, KVW = c.SW, c.AW, c.KVW
    nc = bass.Bass("TRN2", target_bir_lowering=False)
    es = ExitStack()
    S = Sched(nc, es)

    def din(name, shape, dt=F32):
        return nc.dram_tensor(name, list(shape), dt, kind="ExternalInput").ap()

    x_d = din("x", [TOK, D]); xh_d = din("xh", [P, D]); pos_d = din("pos", [32, TOK + P], I32)
    cT_d = din("cT", [P, KC]); wada_d = din("w_ada", [D, 6 * D]); bada_d = din("b_adaT", [P, 6 * KC])
    gpm_d = din("g_pre_mixT", [P, KC]); gpo_d = din("g_post_mixT", [P, KC])
    gpf_d = din("g_pre_ffnT", [P, KC]); gpof_d = din("g_post_ffnT", [P, KC])
    win_d = din("w_in", [D, c.INW]); sink_d = din("sinkB", [P, NQ])
    lng_d = din("ln_gT", [P, G]); lnb_d = din("ln_bT", [P, G])
    sguw_d = din("sgu_w", [G, P, P]); sgub_d = din("sgu_bB", [P, G * P])
    wps_d = din("w_proj_sgu", [SW, D]); wpa_d = din("w_proj_attn", [AW, D]); wout_d = din("w_out", [D, D])
    wg_d = din("w_gate", [D, c.HID]); wu_d = din("w_up", [D, c.HID]); wd_d = din("w_down", [c.HID, D])
    ident_d = din("identF", [P, P]); rt_d = din("rotT", [P, P]); band_d = din("maskB", [P, 256])
    mask0_d = din("mask0", [P, 256]); tri_d = din("tri", [P, P]); invf_d = din("invf", [32, 1])
    out_d = nc.dram_tensor("out", [TOK, D], F32, kind="ExternalOutput").ap()
    ggd_d = nc.dram_tensor("ggd", [2 * P, D], F32, kind="Internal").ap()

    YB = max(NCH * D * 4, 65536 if D >= 4096 else 0)
    off = 0
    def take(n):
        nonlocal off
        o = off; off += (n + 31) // 32 * 32
        return o
    o_vnq = take(max(NCH * SW * 2, NQ * TT * 2))
    o_k = take(NKV * 640 * 2)
    o_v = take(5 * KVW * 2)
    o_cs = take(2 * 640 * 4)
    o_scr = take(16 * 1024)
    o_oT = take(NQ * TT * 2)
    YB = max(YB, off)
    HB_ = KC * TT * 2
    MB = max(KC * TT * 2, D * 2 + D * 4, 2 * 8 * TT * 2)
    UB = max(G * TT * 2, 16384)

    def sb(name, nbytes):
        return es.enter_context(nc.sbuf_tensor(name, [P, (nbytes + 1) // 2], BF16))
    Y_t = sb("Y", YB); H_t = sb("H", HB_); M_t = sb("M", MB); U_t = sb("U", UB)
    R_t = sb("R", NSLOT * SLOT_ELEMS * 2)
    C_t = sb("C", 24 * 1024)
    tens = {"Y": Y_t, "H": H_t, "M": M_t, "U": U_t, "R": R_t, "C": C_t}

    def V(region, lo, shape, dt, name="", npart=P):
        esz = 2 if dt == BF16 else 4
        n = int(np.prod(shape))
        a = tens[region][0:npart, lo // 2:(lo + n * esz) // 2]
        if dt != BF16:
            a = a.bitcast(dt)
        if len(shape) == 2:
            a = a.rearrange("p (a b) -> p a b", b=shape[1])
        elif len(shape) == 3:
            a = a.rearrange("p (a b c) -> p a b c", b=shape[1], c=shape[2])
        return TV(a, S.buf(region, lo, lo + n * esz, name))

    def sub(tv, lo_bytes, nbytes, name=""):
        return S.buf(tv.b.region, tv.b.lo + lo_bytes, tv.b.lo + lo_bytes + nbytes, name)

    coff = 0
    def ctake(shape, dt, name, npart=P):
        nonlocal coff
        esz = 2 if dt == BF16 else 4
        n = int(np.prod(shape)) * esz
        t = V("C", coff, shape, dt, name, npart)
        coff += (n + 31) // 32 * 32
        return t
    identF = ctake([P], F32, "identF"); identB = ctake([P], BF16, "identB"); onesF = ctake([P], F32, "onesF")
    rotB = ctake([P], BF16, "rotB"); maskB = ctake([256], F32, "maskB"); mask0 = ctake([256], F32, "mask0")
    invf = ctake([1], F32, "invf", 32); sinkB = ctake([NQ], F32, "sinkB"); nsink = ctake([NQ], F32, "nsink")
    lng = ctake([G], F32, "lng"); lnb = ctake([G], F32, "lnb")
    WmT = ctake([G, P], BF16, "WmT"); bias2 = ctake([G, P], F32, "bias2")
    cT = ctake([KC], F32, "cT"); cact = ctake([KC], F32, "cact"); cactB = ctake([KC], BF16, "cactB")
    bada = ctake([6 * KC], F32, "bada"); modT = ctake([6 * KC], F32, "modT")
    gpm = ctake([KC], F32, "gpm"); gpo = ctake([KC], F32, "gpo"); gpf = ctake([KC], F32, "gpf"); gpof = ctake([KC], F32, "gpof")
    a1 = ctake([KC], F32, "a1"); a2 = ctake([KC], F32, "a2"); gg1 = ctake([KC], F32, "gg1"); gg2 = ctake([KC], F32, "gg2")
    epsR = ctake([1], F32, "epsR"); epsL = ctake([1], F32, "epsL")
    ssq = ctake([8], F32, "ssq"); rstd = ctake([8], F32, "rstd")
    ssqp = ctake([NCH, NCG], F32, "ssqp")
    stats = ctake([NCH, c.NSG * 6], F32, "stats"); mv = ctake([NCH, 2], F32, "mv")
    lnr = ctake([NCH], F32, "lnr"); lnm = ctake([NCH], F32, "lnm")
    mx = ctake([8], F32, "mx"); negm = ctake([8], F32, "negm"); rs = ctake([8], F32, "rs"); esk = ctake([8], F32, "esk")
    den = ctake([8], F32, "den")
    khalo = ctake([NKV, P], BF16, "khalo"); vhalo = ctake([KVW], BF16, "vhalo")
    assert coff <= 24 * 1024, coff

    PS = []
    for i in range(8):
        t = es.enter_context(nc.psum_tensor("ps%d" % i, [P, 512], F32))
        PS.append(TV(t[:, :], S.buf("PS", i * 2048, (i + 1) * 2048, "ps%d" % i)))

    slots = [V("R", i * SLOT_ELEMS * 2, [SLOT_ELEMS], BF16, "slot%d" % i) for i in range(NSLOT)]
    slot_ctr = [0]

    def load_slab(W, k0, nk, c0, ncols):
        assert nk * ncols <= SLOT_ELEMS
        i = slot_ctr[0] % NSLOT; slot_ctr[0] += 1
        sl = slots[i]
        dst = sl.ap[:, 0:nk * ncols].rearrange("p (a b) -> p a b", b=ncols)
        src = W[k0 * P:(k0 + nk) * P, c0:c0 + ncols].rearrange("(a p) n -> p a n", p=P)
        S.dma("pool", ("slot", i), lambda: nc.gpsimd.dma_start(out=dst, in_=src), reads=[], writes=[sl.b])
        return dst, sl.b

    def ckpt(n):
        if stop is not None and n == stop:
            raise _Stop()

    out_b = [[[S.buf("OUT", ((ti * NCH + ch) * NCG + cg), ((ti * NCH + ch) * NCG + cg) + 1) for cg in range(NCG)]
              for ch in range(NCH)] for ti in range(NT)]
    try:
        def cload(tv, src):
            S.dma("sp", "const", lambda: nc.sync.dma_start(out=tv.ap, in_=src), reads=[], writes=[tv.b])
        cload(identF, ident_d); cload(maskB, band_d); cload(mask0, mask0_d)
        cload(invf, invf_d); cload(sinkB, sink_d); cload(lng, lng_d); cload(lnb, lnb_d)
        cload(cT, cT_d); cload(bada, bada_d); cload(gpm, gpm_d); cload(gpo, gpo_d); cload(gpf, gpf_d); cload(gpof, gpof_d)
        sguw = V("Y", 0, [G, P], F32, "sguw")
        sgub = V("Y", G * P * 4, [G, P], F32, "sgub")
        tri = V("Y", 2 * G * P * 4, [P], F32, "tri")
        wmf = V("Y", 2 * G * P * 4 + 512, [P], F32, "wmf")
        S.dma("sp", "const", lambda: nc.sync.dma_start(out=sguw.ap, in_=sguw_d.rearrange("g t s -> t g s")), writes=[sguw.b])
        cload(sgub, sgub_d.rearrange("p (g t) -> p g t", t=P)); cload(tri, tri_d)
        rotF = V("Y", 2 * G * P * 4 + 1024, [P], F32, "rotF")
        cload(rotF, rt_d)
        csem, ctot = S.dsems["const"]
        for r in S.regions.values():
            for b in r:
                if b.last_w is not None and b.last_w[2] == ("dma", "const"):
                    b.last_w = (csem, ctot, ("dma", "const"))

        dve = nc.vector; act = nc.scalar; pe = nc.tensor

        S.op("dve", lambda: dve.tensor_copy(out=identB.ap, in_=identF.ap), [identF.b], [identB.b])
        S.op("dve", lambda: dve.tensor_copy(out=rotB.ap, in_=rotF.ap), [rotF.b], [rotB.b])
        S.op("dve", lambda: dve.memset(onesF.ap, 1.0), [], [onesF.b])
        S.op("dve", lambda: dve.memset(epsR.ap, RMS_EPS), [], [epsR.b])
        S.op("dve", lambda: dve.memset(epsL.ap, LN_EPS), [], [epsL.b])
        S.op("dve", lambda: dve.tensor_scalar(out=nsink.ap, in0=sinkB.ap, scalar1=-1.0, scalar2=None, op0=ALU.mult),
             [sinkB.b], [nsink.b])
        S.op("act", lambda: act.activation(out=cact.ap, in_=cT.ap, func=AF.Silu), [cT.b], [cact.b])
        ckpt(1)

        for g in range(G):
            pa = PS[(2 * g) % 8]; pb = PS[(2 * g + 1) % 8]
            S.op("pe", lambda: pe.transpose(out=pa.ap[:, 0:P], in_=sguw.ap[:, g, :], identity=identF.ap),
                 [sguw.b, identF.b], [pa.b])
            S.op("dve", lambda: dve.tensor_tensor(out=wmf.ap, in0=pa.ap[:, 0:P], in1=tri.ap, op=ALU.mult),
                 [pa.b, tri.b], [wmf.b])
            S.op("act", lambda: act.copy(out=WmT.ap[:, g, :], in_=wmf.ap), [wmf.b], [WmT.b])
            S.op("pe", lambda: pe.matmul(pb.ap[:, 0:P], lhsT=onesF.ap, rhs=wmf.ap, start=True, stop=True),
                 [onesF.b, wmf.b], [pb.b])
            S.op("dve", lambda: dve.scalar_tensor_tensor(out=bias2.ap[:, g, :], in0=pb.ap[:, 0:P], scalar=lnb.ap[:, g:g + 1],
                                                         in1=sgub.ap[:, g, :], op0=ALU.mult, op1=ALU.add),
                 [pb.b, lnb.b, sgub.b], [bias2.b])

        ckpt(2)
        S.op("dve", lambda: dve.tensor_copy(out=cactB.ap, in_=cact.ap), [cact.b], [cactB.b])
        NQD = 6 * KC // 4
        nkA = min(KC, SLOT_ELEMS // 512)
        psmq = [PS[4 + cc] for cc in range(4)]
        for qd in range(NQD):
            for s0 in range(0, KC, nkA):
                nke = min(nkA, KC - s0)
                slab, slb = load_slab(wada_d, s0, nke, qd * 512, 512)

                def emit():
                    ins = None
                    for cc in range(4):
                        for kl in range(nke):
                            kc = s0 + kl
                            ins = pe.matmul(psmq[cc].ap[:, qd:qd + 1], lhsT=slab[:, kl, cc * P:(cc + 1) * P], rhs=cactB.ap[:, kc:kc + 1],
                                            start=(kc == 0), stop=(kc == KC - 1))
                    return ins
                S.op("pe", emit, [slb, cactB.b], [b_.b for b_ in psmq])
        modT_v = modT.ap.rearrange("p (q c) -> p q c", c=4)
        bada_v = bada.ap.rearrange("p (q c) -> p q c", c=4)
        for cc in range(4):
            S.op("dve", lambda: dve.tensor_tensor(out=modT_v[:, :, cc], in0=psmq[cc].ap[:, 0:NQD], in1=bada_v[:, :, cc], op=ALU.add),
                 [psmq[cc].b, bada.b], [modT.b])
        def mk_a(dst, gvec, sc_lo):
            S.op("dve", lambda: dve.scalar_tensor_tensor(out=dst.ap, in0=modT.ap[:, sc_lo:sc_lo + KC], scalar=1.0, in1=gvec.ap,
                                                         op0=ALU.add, op1=ALU.mult), [modT.b, gvec.b], [dst.b])
        mk_a(a1, gpm, KC); mk_a(a2, gpf, 4 * KC)
        S.op("dve", lambda: dve.tensor_tensor(out=gg1.ap, in0=modT.ap[:, 2 * KC:3 * KC], in1=gpo.ap, op=ALU.mult),
             [modT.b, gpo.b], [gg1.b])
        S.op("dve", lambda: dve.tensor_tensor(out=gg2.ap, in0=modT.ap[:, 5 * KC:6 * KC], in1=gpof.ap, op=ALU.mult),
             [modT.b, gpof.b], [gg2.b])
        sh1 = TV(modT.ap[:, 0:KC], modT.b); sh2 = TV(modT.ap[:, 3 * KC:4 * KC], modT.b)
        ckpt(3)

        Ycc = [[V("Y", (ch * D + cg * 512) * 4, [512], F32, "Y%d_%d" % (ch, cg)) for cg in range(NCG)] for ch in range(NCH)]
        Ych = [tens["Y"][:, ch * D * 2:(ch + 1) * D * 2].bitcast(F32) for ch in range(NCH)]
        def Yb(ch):
            return [t.b for t in Ycc[ch]]
        hT = V("H", 0, [KC, TT], BF16, "hT")
        mergedT = V("M", 0, [KC, TT], BF16, "mergedT")
        junk = V("M", 0, [D], BF16, "junk")
        xhalo = V("M", D * 2, [D], F32, "xhalo")
        hTh = V("U", 0, [KC, P], BF16, "hTh")
        u_ap = tens["U"][:, 0:G * TT].rearrange("p (g t) -> p g t", t=TT)
        u_b = [S.buf("U", g * TT * 2, (g + 1) * TT * 2, "u%d" % g) for g in range(G)]
        xP = [V("U", i * 2048, [512], F32, "xP%d" % i) for i in range(4)]
        ggP = [V("U", 8192 + i * 2048, [512], F32, "ggP%d" % i) for i in range(2)]
        junk2 = V("U", 12288, [512], BF16, "junk2")
        rep = V("U", 13312, [P], F32, "rep")
        vn_ap = tens["Y"][:, o_vnq // 2:o_vnq // 2 + NCH * SW].rearrange("p (c s) -> p c s", s=SW)
        vn_b = [S.buf("Y", o_vnq + ch * SW * 2, o_vnq + (ch + 1) * SW * 2, "vn%d" % ch) for ch in range(NCH)]
        qT_ap = tens["Y"][:, o_vnq // 2:o_vnq // 2 + NQ * TT].rearrange("p (h t) -> p h t", t=TT)
        qT_b = [S.buf("Y", o_vnq + h * TT * 2, o_vnq + (h + 1) * TT * 2, "q%d" % h) for h in range(NQ)]
        kT_ap = tens["Y"][:, o_k // 2:o_k // 2 + NKV * 640].rearrange("p (h t) -> p h t", t=640)
        kT_b = [S.buf("Y", o_k + h * 1280, o_k + (h + 1) * 1280, "k%d" % h) for h in range(NKV)]
        v_ap = tens["Y"][:, o_v // 2:o_v // 2 + 5 * KVW].rearrange("p (c w) -> p c w", w=KVW)
        v_b = S.buf("Y", o_v, o_v + 5 * KVW * 2, "v")
        Ccs = V("Y", o_cs, [640], F32, "Ccs", 32); Scs = V("Y", o_cs + 2560, [640], F32, "Scs", 32)
        sm2 = [V("Y", o_scr + i * 4096, [4, 256], F32, "sm%d" % i) for i in range(2)]
        pn2 = [V("Y", o_scr + 8192 + i * 2048, [4, 256], BF16, "pn%d" % i) for i in range(2)]
        pT = V("Y", o_scr + 12288, [8, P], BF16, "pT")
        gtmp = [V("Y", o_scr + i * 2048, [512], F32, "gtmp%d" % i) for i in range(2)]
        stmp = V("Y", o_scr + 4096, [512], F32, "stmp")
        ropef = V("Y", o_scr, [512], F32, "ropef", 32); ropet = V("Y", o_scr + 2048, [512], F32, "ropet", 32)
        ang = V("Y", o_scr + 4096, [640], F32, "ang", 32); angi = V("Y", o_scr + 8192, [640], I32, "angi", 32)
        sga = V("Y", o_scr, [4, 512], F32, "sga"); sgb = V("Y", o_scr + 8192, [4, 512], F32, "sgb")
        sga_b = [S.buf("Y", o_scr + i * 2048, o_scr + (i + 1) * 2048, "sga%d" % i) for i in range(4)]
        sgb_b = [S.buf("Y", o_scr + 8192 + i * 2048, o_scr + 8192 + (i + 1) * 2048, "sgb%d" % i) for i in range(4)]
        oT_ap = tens["Y"][:, o_oT // 2:o_oT // 2 + NQ * TT].rearrange("p (h t) -> p h t", t=TT)
        oT_b = [S.buf("Y", o_oT + hk * 4 * TT * 2, o_oT + (hk + 1) * 4 * TT * 2, "oT%d" % hk) for hk in range(NKV)]
        actb = [V("M", i * 8 * TT * 2, [8, TT], BF16, "actb%d" % i) for i in range(2)]
        sgt = [V("U", i * 2048, [512], F32, "sgt%d" % i) for i in range(4)]

        SCALE = float(128 ** -0.5)
        evac_rr = [0]

        def fm_group(W, col0, ncols, Kc, rhs_fn, rhs_bufs, banks, N=TT, rhs2_fn=None, rhs2_bufs=(), banks2=None, N2=P, krow0=0):
            nk = min(Kc, SLOT_ELEMS // ncols)
            ncc = ncols // P
            for s0 in range(0, Kc, nk):
                nke = min(nk, Kc - s0)
                slab, slb = load_slab(W, krow0 + s0, nke, col0, ncols)

                def emit():
                    ins = None
                    for cc in range(ncc):
                        for kl in range(nke):
                            kc = s0 + kl
                            ins = pe.matmul(banks[cc].ap[:, 0:N], lhsT=slab[:, kl, cc * P:(cc + 1) * P], rhs=rhs_fn(kc),
                                            start=(kc == 0), stop=(kc == Kc - 1))
                        if rhs2_fn is not None:
                            for kl in range(nke):
                                kc = s0 + kl
                                ins = pe.matmul(banks2[cc].ap[:, 0:N2], lhsT=slab[:, kl, cc * P:(cc + 1) * P], rhs=rhs2_fn(kc),
                                                start=(kc == 0), stop=(kc == Kc - 1))
                    return ins
                wr = [b.b for b in banks[:ncc]] + ([b.b for b in banks2[:ncc]] if rhs2_fn is not None else [])
                S.op("pe", emit, [slb] + list(rhs_bufs) + list(rhs2_bufs), wr)

        def tm_group(W, krow0, Kc, col0, ncols, lhs_fn, lhs_bufs, banks, chunks=range(NCH), lhs2_fn=None, lhs2_bufs=(), bank2=None):
            nk = min(Kc, SLOT_ELEMS // ncols)
            for s0 in range(0, Kc, nk):
                nke = min(nk, Kc - s0)
                slab, slb = load_slab(W, krow0 + s0, nke, col0, ncols)

                for ch in chunks:
                    def emit():
                        ins = None
                        for kl in range(nke):
                            kc = s0 + kl
                            ins = pe.matmul(banks[ch].ap[:, 0:ncols], lhsT=lhs_fn(kc, ch), rhs=slab[:, kl, :],
                                            start=(kc == 0), stop=(kc == Kc - 1))
                        return ins
                    S.op("pe", emit, [slb] + list(lhs_bufs), [banks[ch].b])
                if lhs2_fn is not None:
                    def emit():
                        ins = None
                        for kl in range(nke):
                            kc = s0 + kl
                            ins = pe.matmul(bank2.ap[:, 0:ncols], lhsT=lhs2_fn(kc), rhs=slab[:, kl, :],
                                            start=(kc == 0), stop=(kc == Kc - 1))
                        return ins
                    S.op("pe", emit, [slb] + list(lhs2_bufs), [bank2.b])

        def build_h(src_ap, src_bufs, dst_ap, dst_buf, avec, bvec, sidx, tcol0, ncols_tok=P):
            sq = TV(ssq.ap[:, sidx:sidx + 1], ssq.b); rsd = TV(rstd.ap[:, sidx:sidx + 1], rstd.b)
            S.op("act", lambda: act.activation(out=junk.ap, in_=src_ap, func=AF.Square, accum_out=sq.ap),
                 list(src_bufs), [junk.b, sq.b])
            S.op("act", lambda: act.activation(out=rsd.ap, in_=sq.ap, func=AF.Sqrt, bias=epsR.ap, scale=1.0 / D), [sq.b, epsR.b], [rsd.b])
            S.op("dve", lambda: dve.reciprocal(out=rsd.ap, in_=rsd.ap), [rsd.b], [rsd.b])
            S.op("dve", lambda: dve.tensor_scalar(out=src_ap, in0=src_ap, scalar1=rsd.ap, scalar2=None, op0=ALU.mult),
                 list(src_bufs) + [rsd.b], list(src_bufs))
            for kg in range(KC // 4):
                pb = PS[kg % 2]

                def emit():
                    ins = None
                    for j in range(4):
                        kc = kg * 4 + j
                        ins = pe.transpose(out=pb.ap[:, j * P:(j + 1) * P], in_=src_ap[:, kc * P:(kc + 1) * P], identity=identF.ap)
                    return ins
                S.op("pe", emit, list(src_bufs) + [identF.b], [pb.b])
                for j in range(4):
                    kc = kg * 4 + j
                    o_ = dst_ap[:, kc, tcol0:tcol0 + P]
                    i_ = pb.ap[:, j * P:(j + 1) * P]
                    if kg % 2 == 0:
                        S.op("act", lambda: act.activation(out=o_, in_=i_, func=AF.Identity, bias=bvec.ap[:, kc:kc + 1],
                                                           scale=avec.ap[:, kc:kc + 1]), [pb.b, avec.b, bvec.b], [dst_buf])
                    else:
                        S.op("dve", lambda: dve.tensor_scalar(out=o_, in0=i_, scalar1=avec.ap[:, kc:kc + 1], scalar2=bvec.ap[:, kc:kc + 1],
                                                              op0=ALU.mult, op1=ALU.add), [pb.b, avec.b, bvec.b], [dst_buf])

        ssq_c = [S.buf("C", ssq.b.lo + 4 * i, ssq.b.lo + 4 * i + 4, "ssq%d" % i) for i in range(8)]
        rstd_c = [S.buf("C", rstd.b.lo + 4 * i, rstd.b.lo + 4 * i + 4, "rstd%d" % i) for i in range(8)]
        hTk = [S.buf("H", kc * TT * 2, (kc + 1) * TT * 2, "hT%d" % kc) for kc in range(KC)]

        def build_h_tile(avec, bvec):
            for ch in range(NCH):
                sqa = ssq.ap[:, ch:ch + 1]; rsa = rstd.ap[:, ch:ch + 1]
                S.op("act", lambda: act.activation(out=junk.ap, in_=Ych[ch], func=AF.Square, accum_out=sqa), Yb(ch), [junk.b, ssq_c[ch]])
                S.op("act", lambda: act.activation(out=rsa, in_=sqa, func=AF.Sqrt, bias=epsR.ap, scale=1.0 / D), [ssq_c[ch], epsR.b], [rstd_c[ch]])
                S.op("dve", lambda: dve.reciprocal(out=rsa, in_=rsa), [rstd_c[ch]], [rstd_c[ch]])
                S.op("dve", lambda: dve.tensor_scalar(out=Ych[ch], in0=Ych[ch], scalar1=rsa, scalar2=None, op0=ALU.mult),
                     Yb(ch) + [rstd_c[ch]], Yb(ch))
            allY = [b_ for ch in range(NCH) for b_ in Yb(ch)]
            for kc in range(KC):
                pb = PS[kc % 4]

                def emit():
                    ins = None
                    for ch in range(NCH):
                        ins = pe.transpose(out=pb.ap[:, ch * P:(ch + 1) * P], in_=Ych[ch][:, kc * P:(kc + 1) * P], identity=identF.ap)
                    return ins
                S.op("pe", emit, allY + [identF.b], [pb.b])
                if kc % 2 == 0:
                    S.op("act", lambda: act.activation(out=hT.ap[:, kc, :], in_=pb.ap, func=AF.Identity, bias=bvec.ap[:, kc:kc + 1],
                                                       scale=avec.ap[:, kc:kc + 1]), [pb.b, avec.b, bvec.b], [hTk[kc]])
                else:
                    S.op("dve", lambda: dve.tensor_scalar(out=hT.ap[:, kc, :], in0=pb.ap, scalar1=avec.ap[:, kc:kc + 1], scalar2=bvec.ap[:, kc:kc + 1],
                                                          op0=ALU.mult, op1=ALU.add), [pb.b, avec.b, bvec.b], [hTk[kc]])

        def rope_evac(bank, dstT_ap, dst_buf, col0, N, cs_col0, rbank):
            S.op("act", lambda: act.copy(out=dstT_ap[:, col0:col0 + N], in_=bank.ap[:, 0:N]), [bank.b], [dst_buf])
            S.op("pe", lambda: pe.matmul(rbank.ap[:, 0:N], lhsT=rotB.ap, rhs=dstT_ap[:, col0:col0 + N], start=True, stop=True),
                 [rotB.b, dst_buf], [rbank.b])
            S.op("dve", lambda: dve.tensor_tensor(out=ropet.ap[:, 0:N], in0=rbank.ap[0:32, 0:N], in1=Scs.ap[:, cs_col0:cs_col0 + N], op=ALU.mult),
                 [rbank.b, Scs.b], [ropet.b])
            S.op("dve", lambda: dve.tensor_tensor(out=ropef.ap[:, 0:N], in0=bank.ap[0:32, 0:N], in1=Ccs.ap[:, cs_col0:cs_col0 + N], op=ALU.mult),
                 [bank.b, Ccs.b], [ropef.b])
            S.op("dve", lambda: dve.tensor_tensor(out=dstT_ap[0:32, col0:col0 + N], in0=ropef.ap[:, 0:N], in1=ropet.ap[:, 0:N], op=ALU.add),
                 [ropef.b, ropet.b], [dst_buf])

        def gen_ggP(ggvec, cg, dst):
            pb = PS[6 + (cg % 2)]
            for jj in range(4):
                col = cg * 4 + jj
                S.op("dve", lambda: dve.tensor_scalar(out=rep.ap, in0=onesF.ap, scalar1=ggvec.ap[:, col:col + 1], scalar2=None, op0=ALU.mult),
                     [onesF.b, ggvec.b], [rep.b])
                S.op("pe", lambda: pe.matmul(pb.ap[:, jj * P:(jj + 1) * P], lhsT=rep.ap, rhs=identF.ap, start=True, stop=True),
                     [rep.b, identF.b], [pb.b])
            S.op("act", lambda: act.copy(out=dst.ap, in_=pb.ap), [pb.b], [dst.b])

        def combine(ti, gv, src_is_x):
            S.op("dve", lambda: dve.tensor_reduce(out=ssq.ap[:, 0:NCH], in_=ssqp.ap, axis=AX.X, op=ALU.add), [ssqp.b], [ssq.b])
            S.op("act", lambda: act.activation(out=rstd.ap[:, 0:NCH], in_=ssq.ap[:, 0:NCH], func=AF.Sqrt, bias=epsR.ap, scale=1.0 / D),
                 [ssq.b, epsR.b], [rstd.b])
            S.op("dve", lambda: dve.reciprocal(out=rstd.ap[:, 0:NCH], in_=rstd.ap[:, 0:NCH]), [rstd.b], [rstd.b])
            ckpt(132)
            items = [(cg, ch) for cg in range(NCG) for ch in range(NCH)]

            def issue_load(k):
                cg, ch = items[k]
                xp = xP[k % 4]
                r0 = ti * TT + ch * P
                src = (x_d if src_is_x else out_d)[r0:r0 + P, cg * 512:(cg + 1) * 512]
                S.dma("sp", ("xp", k % 4), lambda: nc.sync.dma_start(out=xp.ap, in_=src),
                      reads=([] if src_is_x else [out_b[ti][ch][cg]]), writes=[xp.b])
            def issue_gg(cg):
                gp_ = ggP[cg % 2]
                S.dma("sp", ("ggld", cg % 2), lambda: nc.sync.dma_start(out=gp_.ap, in_=ggd_d[gv * P:(gv + 1) * P, cg * 512:(cg + 1) * 512]),
                      reads=[ggd_b[gv][cg]], writes=[gp_.b])
            issue_gg(0)
            if NCG > 1:
                issue_gg(1)
            for k in range(min(3, len(items))):
                issue_load(k)
            for k, (cg, ch) in enumerate(items):
                gp = ggP[cg % 2]
                if ch == 0 and cg >= 1 and cg + 1 < NCG:
                    issue_gg(cg + 1)
                xp = xP[k % 4]
                r0 = ti * TT + ch * P
                yb = Ycc[ch][cg]
                ob = out_b[ti][ch][cg]
                S.op("dve", lambda: dve.scalar_tensor_tensor(out=yb.ap, in0=yb.ap, scalar=rstd.ap[:, ch:ch + 1], in1=gp.ap,
                                                             op0=ALU.mult, op1=ALU.mult), [yb.b, rstd.b, gp.b], [yb.b])
                S.op("dve", lambda: dve.tensor_tensor(out=yb.ap, in0=yb.ap, in1=xp.ap, op=ALU.add), [yb.b, xp.b], [yb.b])
                S.dma("sp", ("st", ch, cg), lambda: nc.sync.dma_start(out=out_d[r0:r0 + P, cg * 512:(cg + 1) * 512], in_=yb.ap),
                      reads=[yb.b], writes=[ob])
                ckpt(134)
                if k + 3 < len(items):
                    issue_load(k + 3)

        ggd_b = [[S.buf("GGD", v * NCG + cg, v * NCG + cg + 1) for cg in range(NCG)] for v in range(2)]
        for v, ggvec in enumerate((gg1, gg2)):
            for cg in range(NCG):
                gp = ggP[cg % 2]
                gen_ggP(ggvec, cg, gp)
                S.dma("sp", ("ggst", cg % 2), lambda: nc.sync.dma_start(out=ggd_d[v * P:(v + 1) * P, cg * 512:(cg + 1) * 512], in_=gp.ap),
                      reads=[gp.b], writes=[ggd_b[v][cg]])

        for ti in range(NT):
            t0 = ti * TT
            if ti == 0:
                S.dma("sp", "xh", lambda: nc.sync.dma_start(out=xhalo.ap, in_=xh_d[:, :]), reads=[], writes=[xhalo.b])
                build_h(xhalo.ap, [xhalo.b], hTh.ap, hTh.b, a1, sh1, 4, 0)
            for ch in range(NCH):
                r0 = t0 + ch * P
                S.dma("sp", ("xl", ch), lambda: nc.sync.dma_start(out=Ych[ch], in_=x_d[r0:r0 + P, :]), reads=[], writes=Yb(ch))
            build_h_tile(a1, sh1)

            if ti == 0: ckpt(4)
            S.dma("sp", "pos", lambda: nc.sync.dma_start(out=angi.ap, in_=pos_d[:, t0:t0 + 640]),
                  reads=[], writes=[angi.b])
            S.op("dve", lambda: dve.tensor_copy(out=ang.ap, in_=angi.ap), [angi.b], [ang.b])
            S.op("dve", lambda: dve.tensor_scalar(out=ang.ap, in0=ang.ap, scalar1=invf.ap, scalar2=None, op0=ALU.mult), [ang.b, invf.b], [ang.b])
            S.op("dve", lambda: dve.tensor_scalar(out=angi.ap, in0=ang.ap, scalar1=float(1.0 / (2 * np.pi)), scalar2=None, op0=ALU.mult), [ang.b], [angi.b])
            S.op("dve", lambda: dve.tensor_copy(out=Scs.ap, in_=angi.ap), [angi.b], [Scs.b])
            S.op("dve", lambda: dve.scalar_tensor_tensor(out=ang.ap, in0=Scs.ap, scalar=float(-2 * np.pi), in1=ang.ap, op0=ALU.mult, op1=ALU.add),
                 [Scs.b, ang.b], [ang.b])
            S.op("act", lambda: act.activation(out=Scs.ap, in_=ang.ap, func=AF.Sin, scale=0.5), [ang.b], [Scs.b])
            S.op("act", lambda: act.activation(out=Ccs.ap, in_=ang.ap, func=AF.Sin, scale=0.25), [ang.b], [Ccs.b])
            S.op("dve", lambda: dve.tensor_tensor(out=Ccs.ap, in0=Ccs.ap, in1=Ccs.ap, op=ALU.mult), [Ccs.b], [Ccs.b])
            S.op("dve", lambda: dve.tensor_scalar(out=Ccs.ap, in0=Ccs.ap, scalar1=-2.0, scalar2=1.0, op0=ALU.mult, op1=ALU.add), [Ccs.b], [Ccs.b])
            S.op("dve", lambda: dve.tensor_tensor(out=ang.ap, in0=Scs.ap, in1=Scs.ap, op=ALU.mult), [Scs.b], [ang.b])
            S.op("dve", lambda: dve.scalar_tensor_tensor(out=Scs.ap, in0=Scs.ap, scalar=2.0, in1=Ccs.ap, op0=ALU.mult, op1=ALU.mult),
                 [Scs.b, Ccs.b], [Scs.b])
            S.op("dve", lambda: dve.tensor_scalar(out=Ccs.ap, in0=ang.ap, scalar1=-2.0, scalar2=1.0, op0=ALU.mult, op1=ALU.add), [ang.b], [Ccs.b])

            if ti == 0: ckpt(5)
            hT_rhs = lambda kc: hT.ap[:, kc, :]
            if ti > 0:
                for hk in range(NKV):
                    S.op("dve", lambda: dve.tensor_copy(out=kT_ap[:, hk, 0:P], in_=khalo.ap[:, hk, :]), [khalo.b], [kT_b[hk]])
                S.op("dve", lambda: dve.tensor_copy(out=v_ap[:, 0, :], in_=vhalo.ap), [vhalo.b], [v_b])
            kstep = 2 if NKV >= 2 else 1
            for hk0 in range(0, NKV, kstep):
                banks = [PS[2 + i] for i in range(kstep)]; banks2 = [PS[4 + i] for i in range(kstep)]
                fm_group(win_d, c.ok + hk0 * P, kstep * P, KC, hT_rhs, [hT.b], banks,
                         rhs2_fn=(lambda kc: hTh.ap[:, kc, :]) if ti == 0 else None, rhs2_bufs=[hTh.b] if ti == 0 else (), banks2=banks2)
                if ti == 0: ckpt(51)
                for i in range(kstep):
                    hk = hk0 + i
                    rope_evac(banks[i], kT_ap[:, hk], kT_b[hk], P, TT, P, PS[6])
                    if ti == 0: ckpt(52)
                    if ti == 0:
                        rope_evac(banks2[i], kT_ap[:, hk], kT_b[hk], 0, P, 0, PS[7])
            if ti == 0: ckpt(6)
            vb = [PS[ch] for ch in range(NCH)]
            tm_group(win_d, 0, KC, c.ov, KVW, lambda kc, ch: hT.ap[:, kc, ch * P:(ch + 1) * P], [hT.b], vb,
                     lhs2_fn=(lambda kc: hTh.ap[:, kc, :]) if ti == 0 else None, lhs2_bufs=[hTh.b] if ti == 0 else (), bank2=PS[4])
            for ch in range(NCH):
                S.op("act" if ch % 2 == 0 else "dve",
                     (lambda: act.copy(out=v_ap[:, 1 + ch, :], in_=vb[ch].ap[:, 0:KVW])) if ch % 2 == 0 else
                     (lambda: dve.tensor_copy(out=v_ap[:, 1 + ch, :], in_=vb[ch].ap[:, 0:KVW])), [vb[ch].b], [v_b])
            if ti == 0:
                S.op("act", lambda: act.copy(out=v_ap[:, 0, :], in_=PS[4].ap[:, 0:KVW]), [PS[4].b], [v_b])
            if ti + 1 < NT:
                for hk in range(NKV):
                    S.op("dve", lambda: dve.tensor_copy(out=khalo.ap[:, hk, :], in_=kT_ap[:, hk, TT:TT + P]), [kT_b[hk]], [khalo.b])
                S.op("dve", lambda: dve.tensor_copy(out=vhalo.ap, in_=v_ap[:, 4, :]), [v_b], [vhalo.b])

            if ti == 0: ckpt(7)
            for sg in range(c.NSG):
                bk = [PS[4 * (sg % 2) + ch] for ch in range(NCH)]
                tm_group(win_d, 0, KC, c.osv + sg * c.SGW, c.SGW, lambda kc, ch: hT.ap[:, kc, ch * P:(ch + 1) * P], [hT.b], bk)
                for ch in range(NCH):
                    gt = gtmp[ch % 2]
                    S.op("act", lambda: act.activation(out=gt.ap[:, 0:c.SGW], in_=bk[ch].ap[:, 0:c.SGW], func=AF.Gelu), [bk[ch].b], [gt.b])
                    S.op("dve", lambda: dve.bn_stats(out=stats.ap[:, ch, sg * 6:(sg + 1) * 6], in_=gt.ap[:, 0:c.SGW]), [gt.b], [stats.b])
                    S.op("dve", lambda: dve.tensor_copy(out=vn_ap[:, ch, sg * c.SGW:(sg + 1) * c.SGW], in_=gt.ap[:, 0:c.SGW]), [gt.b], [vn_b[ch]])
            for ch in range(NCH):
                S.op("dve", lambda: dve.bn_aggr(out=mv.ap[:, ch, :], in_=stats.ap[:, ch, :]), [stats.b], [mv.b])
            S.op("act", lambda: act.activation(out=lnr.ap, in_=mv.ap[:, :, 1], func=AF.Sqrt, bias=epsL.ap, scale=1.0), [mv.b, epsL.b], [lnr.b])
            S.op("dve", lambda: dve.reciprocal(out=lnr.ap, in_=lnr.ap), [lnr.b], [lnr.b])
            S.op("dve", lambda: dve.scalar_tensor_tensor(out=lnm.ap, in0=mv.ap[:, :, 0], scalar=-1.0, in1=lnr.ap, op0=ALU.mult, op1=ALU.mult),
                 [mv.b, lnr.b], [lnm.b])
            for ch in range(NCH):
                S.op("dve", lambda: dve.tensor_scalar(out=vn_ap[:, ch, :], in0=vn_ap[:, ch, :], scalar1=lnr.ap[:, ch:ch + 1], scalar2=lnm.ap[:, ch:ch + 1],
                                                      op0=ALU.mult, op1=ALU.add), [vn_b[ch], lnr.b, lnm.b], [vn_b[ch]])

            if ti == 0: ckpt(8)
            for gq in range(G // 4):
                banks = [PS[4 * (gq % 2) + i] for i in range(4)]
                fm_group(win_d, c.osu + gq * 512, 512, KC, hT_rhs, [hT.b], banks)
                for i in range(4):
                    g = gq * 4 + i
                    S.op("act", lambda: act.activation(out=u_ap[:, g, :], in_=banks[i].ap, func=AF.Gelu), [banks[i].b], [u_b[g]])

            if ti == 0: ckpt(9)
            for g in range(G):
                pb = PS[g % 4]

                def emit():
                    ins = None
                    for ch in range(NCH):
                        ins = pe.matmul(pb.ap[:, ch * P:(ch + 1) * P], lhsT=vn_ap[:, ch, g * P:(g + 1) * P], rhs=WmT.ap[:, g, :], start=True, stop=True)
                    return ins
                S.op("pe", emit, vn_b + [WmT.b], [pb.b])
                S.op("dve", lambda: dve.scalar_tensor_tensor(out=stmp.ap.rearrange("p (c t) -> p c t", t=P), in0=pb.ap.rearrange("p (c t) -> p c t", t=P),
                                                             scalar=lng.ap[:, g:g + 1], in1=bias2.ap[:, g, :].unsqueeze(1).broadcast_to([P, NCH, P]),
                                                             op0=ALU.mult, op1=ALU.add), [pb.b, lng.b, bias2.b], [stmp.b])
                S.op("dve", lambda: dve.tensor_tensor(out=u_ap[:, g, :], in0=stmp.ap, in1=u_ap[:, g, :], op=ALU.mult), [stmp.b, u_b[g]], [u_b[g]])

            if ti == 0: ckpt(10)
            for hq in range(NQ // 4):
                banks = [PS[4 * (hq % 2) + i] for i in range(4)]
                rbanks = [PS[4 * ((hq + 1) % 2) + i] for i in range(4)]
                fm_group(win_d, c.oq + hq * 512, 512, KC, hT_rhs, [hT.b], banks)
                for i in range(4):
                    h = hq * 4 + i
                    rope_evac(banks[i], qT_ap[:, h], qT_b[h], 0, TT, P, rbanks[i])

            if ti == 0: ckpt(11)
            groups = [(b_, hk_) for b_ in range(NCH) for hk_ in range(NKV)]

            def att_banks(gi):
                par = gi % 2
                return [PS[par * 3 + 0], PS[par * 3 + 1]], PS[par * 3 + 2], PS[6]

            def att_scores(gi):
                b_, hk_ = groups[gi]
                ps_s, _, _ = att_banks(gi)

                def emit():
                    ins = None
                    for hh in range(4):
                        h = hk_ * 4 + hh
                        ins = pe.matmul(ps_s[hh // 2].ap[:, (hh % 2) * 256:(hh % 2) * 256 + 256], lhsT=qT_ap[:, h, b_ * P:(b_ + 1) * P],
                                        rhs=kT_ap[:, hk_, b_ * P:b_ * P + 256], start=True, stop=True)
                    return ins
                S.op("pe", emit, [qT_b[hk_ * 4 + hh] for hh in range(4)] + [kT_b[hk_]], [ps_s[0].b, ps_s[1].b])

            att_scores(0)
            for gi, (b, hk) in enumerate(groups):
                ps_s, ps_t, ps_o = att_banks(gi)
                smx = sm2[gi % 2]; pnx = pn2[gi % 2]
                msk = mask0 if (ti == 0 and b == 0) else maskB
                for h2 in range(2):
                    S.op("dve", lambda: dve.tensor_tensor(out=smx.ap[:, 2 * h2:2 * h2 + 2, :], in0=ps_s[h2].ap.rearrange("p (a k) -> p a k", k=256),
                                                          in1=msk.ap.unsqueeze(1).broadcast_to([P, 2, 256]), op=ALU.add), [ps_s[h2].b, msk.b], [smx.b])
                S.op("dve", lambda: dve.tensor_reduce(out=mx.ap[:, 0:4], in_=smx.ap, axis=AX.X, op=ALU.max), [smx.b], [mx.b])
                S.op("dve", lambda: dve.scalar_tensor_tensor(out=negm.ap[:, 0:4], in0=mx.ap[:, 0:4], scalar=-SCALE, in1=nsink.ap[:, hk * 4:hk * 4 + 4],
                                                             op0=ALU.mult, op1=ALU.min), [mx.b, nsink.b], [negm.b])
                S.op("dve", lambda: dve.tensor_tensor(out=esk.ap[:, 0:4], in0=negm.ap[:, 0:4], in1=sinkB.ap[:, hk * 4:hk * 4 + 4], op=ALU.add),
                     [negm.b, sinkB.b], [esk.b])
                if gi + 1 < len(groups):
                    att_scores(gi + 1)
                for hh in range(4):
                    h = hk * 4 + hh
                    S.op("act", lambda: act.activation(out=smx.ap[:, hh, :], in_=smx.ap[:, hh, :], func=AF.Exp, bias=negm.ap[:, hh:hh + 1], scale=SCALE,
                                                       accum_out=rs.ap[:, hh:hh + 1]), [smx.b, negm.b], [smx.b, rs.b])
                S.op("act", lambda: act.activation(out=esk.ap[:, 0:4], in_=esk.ap[:, 0:4], func=AF.Exp), [esk.b], [esk.b])
                S.op("dve", lambda: dve.tensor_tensor(out=den.ap[:, 0:4], in0=rs.ap[:, 0:4], in1=esk.ap[:, 0:4], op=ALU.add), [rs.b, esk.b], [den.b])
                S.op("dve", lambda: dve.reciprocal(out=den.ap[:, 0:4], in_=den.ap[:, 0:4]), [den.b], [den.b])
                for hh in range(4):
                    S.op("dve", lambda: dve.tensor_scalar(out=pnx.ap[:, hh, :], in0=smx.ap[:, hh, :], scalar1=den.ap[:, hh:hh + 1], scalar2=None, op0=ALU.mult),
                         [smx.b, den.b], [pnx.b])
                pst = ps_t.ap.bitcast(BF16)

                def emit():
                    ins = None
                    for hh in range(4):
                        for k2 in range(2):
                            ins = pe.transpose(out=pst[:, (hh * 2 + k2) * P:(hh * 2 + k2 + 1) * P], in_=pnx.ap[:, hh, k2 * P:(k2 + 1) * P], identity=identB.ap)
                    return ins
                S.op("pe", emit, [pnx.b, identB.b], [ps_t.b])
                S.op("act", lambda: act.copy(out=pT.ap.rearrange("p a b -> p (a b)"), in_=pst), [ps_t.b], [pT.b])

                def emit():
                    ins = None
                    for hh in range(4):
                        for k2 in range(2):
                            ins = pe.matmul(ps_o.ap[:, hh * P:(hh + 1) * P], lhsT=v_ap[:, b + k2, hk * P:(hk + 1) * P], rhs=pT.ap[:, hh * 2 + k2, :],
                                            start=(k2 == 0), stop=(k2 == 1))
                    return ins
                S.op("pe", emit, [v_b, pT.b], [ps_o.b])
                S.op("act", lambda: act.copy(out=oT_ap[:, hk * 4:hk * 4 + 4, b * P:(b + 1) * P], in_=ps_o.ap.rearrange("p (h q) -> p h q", q=P)),
                     [ps_o.b], [oT_b[hk]])

            if ti == 0: ckpt(12)
            for jq in range(KC // 4):
                col0 = jq * 512
                bA = [PS[i] for i in range(4)]; bB = [PS[4 + i] for i in range(4)]
                fm_group(win_d, c.oga + col0, 512, KC, hT_rhs, [hT.b], bA)
                for i in range(4):
                    S.op("act", lambda: act.activation(out=sga.ap[:, i, :], in_=bA[i].ap, func=AF.Sigmoid), [bA[i].b], [sga_b[i]])
                fm_group(win_d, c.ogb + col0, 512, KC, hT_rhs, [hT.b], bB)
                for i in range(4):
                    S.op("act", lambda: act.activation(out=sgb.ap[:, i, :], in_=bB[i].ap, func=AF.Sigmoid), [bB[i].b], [sgb_b[i]])
                fm_group(wps_d, col0, 512, G, lambda kc: u_ap[:, kc, :], u_b, bA)
                for i in range(4):
                    S.op("dve", lambda: dve.tensor_tensor(out=sga.ap[:, i, :], in0=bA[i].ap, in1=sga.ap[:, i, :], op=ALU.mult), [bA[i].b, sga_b[i]], [sga_b[i]])
                fm_group(wpa_d, col0, 512, NQ, lambda kc: oT_ap[:, kc, :], oT_b, bB)
                for i in range(4):
                    S.op("dve", lambda: dve.tensor_tensor(out=sgb.ap[:, i, :], in0=bB[i].ap, in1=sgb.ap[:, i, :], op=ALU.mult), [bB[i].b, sgb_b[i]], [sgb_b[i]])
                    S.op("dve", lambda: dve.tensor_tensor(out=mergedT.ap[:, jq * 4 + i, :], in0=sga.ap[:, i, :], in1=sgb.ap[:, i, :], op=ALU.add),
                         [sga_b[i], sgb_b[i]], [mergedT.b])

            if ti == 0: ckpt(13)
            for cg in range(NCG):
                bk = [PS[4 * (cg % 2) + ch] for ch in range(NCH)]
                tm_group(wout_d, 0, KC, cg * 512, 512, lambda kc, ch: mergedT.ap[:, kc, ch * P:(ch + 1) * P], [mergedT.b], bk)
                for ch in range(NCH):
                    S.op("dve", lambda: dve.tensor_copy(out=Ycc[ch][cg].ap, in_=bk[ch].ap), [bk[ch].b], [Ycc[ch][cg].b])
                    S.op("act", lambda: act.activation(out=junk2.ap, in_=Ycc[ch][cg].ap, func=AF.Square, accum_out=ssqp.ap[:, ch, cg:cg + 1]),
                         [Ycc[ch][cg].b], [junk2.b, ssqp.b])
            if ti == 0: ckpt(131)
            combine(ti, 0, True)

            if ti == 0: ckpt(14)
            build_h_tile(a2, sh2)

            if ti == 0: ckpt(15)
            NHB = (HC + 7) // 8
            for hb in range(NHB):
                nch_ = min(8, HC - hb * 8)
                ab = actb[hb % 2]
                q0 = 0
                while q0 < nch_:
                    w_ = 4 if nch_ - q0 >= 4 else 2
                    col0 = (hb * 8 + q0) * P
                    bG = [PS[i] for i in range(w_)]; bU = [PS[4 + i] for i in range(w_)]
                    fm_group(wg_d, col0, w_ * P, KC, hT_rhs, [hT.b], bG)
                    for i in range(w_):
                        S.op("act", lambda: act.activation(out=sgt[i].ap, in_=bG[i].ap, func=AF.Silu), [bG[i].b], [sgt[i].b])
                    fm_group(wu_d, col0, w_ * P, KC, hT_rhs, [hT.b], bU)
                    for i in range(w_):
                        S.op("dve", lambda: dve.tensor_tensor(out=ab.ap[:, q0 + i, :], in0=sgt[i].ap, in1=bU[i].ap, op=ALU.mult),
                             [sgt[i].b, bU[i].b], [ab.b])
                    q0 += w_
                for cg in range(NCG):
                    bk = [PS[4 * (cg % 2) + ch] for ch in range(NCH)]
                    tm_group(wd_d, hb * 8, nch_, cg * 512, 512, lambda kc, ch: ab.ap[:, kc, ch * P:(ch + 1) * P], [ab.b], bk)
                    for ch in range(NCH):
                        yb = Ycc[ch][cg]
                        if hb == 0:
                            S.op("dve", lambda: dve.tensor_copy(out=yb.ap, in_=bk[ch].ap), [bk[ch].b], [yb.b])
                        else:
                            S.op("dve", lambda: dve.tensor_tensor(out=yb.ap, in0=bk[ch].ap, in1=yb.ap, op=ALU.add), [bk[ch].b, yb.b], [yb.b])
            for cg in range(NCG):
                for ch in range(NCH):
                    S.op("act", lambda: act.activation(out=junk2.ap, in_=Ycc[ch][cg].ap, func=AF.Square, accum_out=ssqp.ap[:, ch, cg:cg + 1]),
                         [Ycc[ch][cg].b], [junk2.b, ssqp.b])
            combine(ti, 1, False)

    except _Stop:
        S.dma('sp', ('st', 0, 0), lambda: nc.sync.dma_start(out=out_d[0:P, 0:512], in_=tens['C'][:, 0:1024].bitcast(F32)), reads=[], writes=[out_b[0][0][0]])
    allout = [out_b[ti][ch][cg] for ti in range(NT) for ch in range(NCH) for cg in range(NCG)]
    S.final_wait("sp", allout)
    es.close()
    return nc


def _consts():
    identF = np.eye(P, dtype=np.float32)
    rotT = np.zeros((P, P), np.float32)
    for m in range(16):
        rotT[m + 16, m] = -1.0
        rotT[m, m + 16] = 1.0
    qi = np.arange(P)[:, None]; ki = np.arange(256)[None, :]
    diff = qi + P - ki
    band = np.where((diff >= 0) & (diff < P), 0.0, NEG).astype(np.float32)
    s_ = np.arange(P)[:, None]; t_ = np.arange(P)[None, :]
    tri = (s_ <= t_).astype(np.float32)
    inv = (np.float32(ROPE_THETA) ** (-np.arange(0, 32, 2, dtype=np.float32) / np.float32(32))).astype(np.float32)
    invf = np.concatenate([inv, inv]).reshape(32, 1).astype(np.float32)
    return identF, rotT, band, tri, invf


def make_in_maps(cfg, inp):
    c = cfg
    f32 = np.float32
    identF, rotT, band, tri, invf = _consts()
    x = np.asarray(inp["x"], f32); cc = np.asarray(inp["c"], f32); pos = np.asarray(inp["positions"]).astype(np.int32)
    fm = lambda v, n: np.ascontiguousarray(np.asarray(v, f32).reshape(n, P).T)
    L = 0
    shared = {
        "w_ada": np.ascontiguousarray(np.asarray(inp["w_ada"], f32)[L]),
        "b_adaT": fm(np.asarray(inp["b_ada"])[L], 6 * c.KC),
        "g_pre_mixT": fm(np.asarray(inp["g_pre_mix"])[L], c.KC), "g_post_mixT": fm(np.asarray(inp["g_post_mix"])[L], c.KC),
        "g_pre_ffnT": fm(np.asarray(inp["g_pre_ffn"])[L], c.KC), "g_post_ffnT": fm(np.asarray(inp["g_post_ffn"])[L], c.KC),
        "w_in": np.ascontiguousarray(np.asarray(inp["w_in"], f32)[L]),
        "sinkB": np.ascontiguousarray(np.broadcast_to(np.asarray(inp["attn_sinks"], f32)[L][None, :], (P, c.NQ))),
        "ln_gT": fm(np.asarray(inp["sgu_ln_g"])[L], c.G), "ln_bT": fm(np.asarray(inp["sgu_ln_b"])[L], c.G),
        "sgu_w": np.ascontiguousarray(np.asarray(inp["sgu_w"], f32)[L]),
        "sgu_bB": np.ascontiguousarray(np.broadcast_to(np.asarray(inp["sgu_b"], f32)[L].reshape(1, -1), (P, c.G * P))),
        "w_proj_sgu": np.ascontiguousarray(np.asarray(inp["w_proj_sgu"], f32)[L]),
        "w_proj_attn": np.ascontiguousarray(np.asarray(inp["w_proj_attn"], f32)[L]),
        "w_out": np.ascontiguousarray(np.asarray(inp["w_out"], f32)[L]),
        "w_gate": np.ascontiguousarray(np.asarray(inp["w_gate"], f32)[L]),
        "w_up": np.ascontiguousarray(np.asarray(inp["w_up"], f32)[L]),
        "w_down": np.ascontiguousarray(np.asarray(inp["w_down"], f32)[L]),
        "identF": identF, "rotT": rotT, "maskB": band, "tri": tri, "invf": invf,
    }
    maps = []
    for i in range(c.NCORES):
        b = i // c.CPB; half = i % c.CPB; t0 = half * c.TOK
        m = dict(shared)
        m["x"] = np.ascontiguousarray(x[b, t0:t0 + c.TOK])
        if half > 0:
            m["xh"] = np.ascontiguousarray(x[b, t0 - P:t0])
            ph = pos[b, t0 - P:t0]
            m["mask0"] = band
        else:
            m["xh"] = np.zeros((P, c.D), f32)
            ph = np.zeros((P,), np.int32)
            mk = band.copy(); mk[:, 0:P] = NEG
            m["mask0"] = mk
        m["pos"] = np.ascontiguousarray(np.broadcast_to(np.concatenate([ph, pos[b, t0:t0 + c.TOK]]).reshape(1, -1).astype(np.int32), (32, c.TOK + P)))
        m["cT"] = fm(cc[b], c.KC)
        maps.append(m)
    return maps


def run(cfg, inp, trace=False, stop=None):
    nc = build_program(cfg, stop)
    maps = make_in_maps(cfg, inp)
    res = run_bass_kernel_spmd(nc, maps, core_ids=list(range(cfg.NCORES)), trace=trace)
    outs = [np.asarray(r["out"]) for r in res.results]
    B = cfg.BATCH
    full = np.stack([np.concatenate(outs[b * cfg.CPB:(b + 1) * cfg.CPB], axis=0) for b in range(B)], axis=0)
    return full.astype(np.float32), res


def kernel(**inputs):
    cfg = Cfg(4096, 4096, 4)
    out, _ = run(cfg, inputs)
    return out
```

```python
import numpy as np
from contextlib import ExitStack
import concourse.bass as bass
import concourse.mybir as mybir
from concourse.bass_utils import run_bass_kernel_spmd

F32 = mybir.dt.float32
BF16 = mybir.dt.bfloat16
I32 = mybir.dt.int32
AF = mybir.ActivationFunctionType
ALU = mybir.AluOpType
AX = mybir.AxisListType

P = 128
TT = 512
NCH = 4
RMS_EPS = 1e-6
LN_EPS = 1e-5
ROPE_THETA = 500000.0
NEG = -30000.0
SLOT_ELEMS = 4096
NSLOT = 4


class Cfg:
    def __init__(s, D, SEQ, BATCH, NCORES=8):
        s.D = D; s.KC = D // 128; s.NQ = D // 256; s.NKV = s.NQ // 4; s.G = D // 256
        s.SW = s.G * 128; s.AW = s.NQ * 128; s.KVW = s.NKV * 128
        s.HID = -(-(8 * D) // (3 * 256)) * 256; s.HC = s.HID // 128
        s.INW = s.AW + 2 * s.KVW + 2 * s.SW + 2 * D
        s.oq = 0; s.ok = s.AW; s.ov = s.ok + s.KVW; s.osu = s.ov + s.KVW
        s.osv = s.osu + s.SW; s.oga = s.osv + s.SW; s.ogb = s.oga + D
        s.SEQ = SEQ; s.BATCH = BATCH; s.NCORES = NCORES
        s.CPB = NCORES // BATCH
        s.TOK = SEQ // s.CPB; s.NT = s.TOK // TT
        s.NCG = D // 512
        s.NSG = max(1, s.SW // 512)
        s.SGW = min(512, s.SW)


class Buf:
    __slots__ = ("region", "lo", "hi", "last_w", "readers", "name")

    def __init__(self, region, lo, hi, name=""):
        self.region = region; self.lo = lo; self.hi = hi
        self.last_w = None
        self.readers = {}
        self.name = name


class Sched:
    def __init__(self, nc, es):
        self.nc = nc; self.es = es
        self.regions = {}
        self.eng = {}
        for name, h in (("pe", nc.tensor), ("act", nc.scalar), ("dve", nc.vector),
                        ("pool", nc.gpsimd), ("sp", nc.sync)):
            sem = es.enter_context(nc.semaphore("s_" + name))
            self.eng[name] = {"h": h, "sem": sem, "cnt": 0, "seen": {}}
        self.dsems = {}
        self.nwaits = 0

    def buf(self, region, lo, hi, name=""):
        b = Buf(region, lo, hi, name)
        self.regions.setdefault(region, []).append(b)
        return b

    def _conf(self, b):
        return [o for o in self.regions[b.region] if o.lo < b.hi and b.lo < o.hi]

    def _collect(self, engkey, reads, writes, is_dma):
        deps = {}

        def add(tok, kind):
            sem, val, ek = tok
            if (not is_dma) and ek == engkey and engkey == "pe" and kind != "raw":
                return
            k = id(sem)
            if k not in deps or deps[k][1] < val:
                deps[k] = (sem, val)
        for b in reads:
            for o in self._conf(b):
                if o.last_w is not None:
                    add(o.last_w, "raw")
                if b.region == "PS":
                    for ek, (sem, val) in o.readers.items():
                        if ek != engkey:
                            add((sem, val, ek), "rar")
        for b in writes:
            for o in self._conf(b):
                if o.last_w is not None:
                    add(o.last_w, "waw")
                for ek, (sem, val) in o.readers.items():
                    add((sem, val, ek), "war")
        return deps

    def _emit_waits(self, qname, deps):
        e = self.eng[qname]
        for k, (sem, val) in deps.items():
            if e["seen"].get(k, 0) >= val:
                continue
            e["h"].wait_ge(sem, val)
            e["seen"][k] = val
            self.nwaits += 1

    def _update(self, tok, reads, writes):
        sem, val, ek = tok
        for b in reads:
            b.readers[ek] = (sem, val)
        for b in writes:
            for o in self._conf(b):
                if o is not b:
                    o.last_w = None
                    o.readers = {}
            b.last_w = tok
            b.readers = {}

    def op(self, ename, emit, reads=(), writes=()):
        e = self.eng[ename]
        deps = self._collect(ename, reads, writes, False)
        self._emit_waits(ename, deps)
        ins = emit()
        e["cnt"] += 1
        ins.then_inc(e["sem"], 1)
        self._update((e["sem"], e["cnt"], ename), reads, writes)

    def dma(self, qname, semkey, emit, reads=(), writes=()):
        if semkey not in self.dsems:
            sem = self.es.enter_context(self.nc.semaphore("d_%d" % len(self.dsems)))
            self.dsems[semkey] = [sem, 0]
        ds = self.dsems[semkey]
        ek = ("dma", semkey)
        deps = self._collect(ek, reads, writes, True)
        self._emit_waits(qname, deps)
        ins = emit()
        ds[1] += 16
        ins.then_inc(ds[0], 16)
        self._update((ds[0], ds[1], ek), reads, writes)

    def final_wait(self, qname, bufs):
        deps = {}
        for b in bufs:
            for o in self._conf(b):
                if o.last_w is not None:
                    sem, val, ek = o.last_w
                    k = id(sem)
                    if k not in deps or deps[k][1] < val:
                        deps[k] = (sem, val)
        self._emit_waits(qname, deps)


class TV:
    __slots__ = ("ap", "b")

    def __init__(self, ap, b):
        self.ap = ap; self.b = b


class _Stop(Exception):
    pass


def build_program(cfg, stop=None):
    c = cfg
    D, KC, NQ, NKV, G, HC, TOK, NT, NCG = c.D, c.KC, c.NQ, c.NKV, c.G, c.HC, c.TOK, c.NT, c.NCG
    SW, AW, KVW = c.SW, c.AW, c.KVW
    nc = bass.Bass("TRN2", target_bir_lowering=False)
    es = ExitStack()
    S = Sched(nc, es)

    def din(name, shape, dt=F32):
        return nc.dram_tensor(name, list(shape), dt, kind="ExternalInput").ap()

    x_d = din("x", [TOK, D]); xh_d = din("xh", [P, D]); pos_d = din("pos", [32, TOK + P], I32)
    cT_d = din("cT", [P, KC]); wada_d = din("w_ada", [D, 6 * D]); bada_d = din("b_adaT", [P, 6 * KC])
    gpm_d = din("g_pre_mixT", [P, KC]); gpo_d = din("g_post_mixT", [P, KC])
    gpf_d = din("g_pre_ffnT", [P, KC]); gpof_d = din("g_post_ffnT", [P, KC])
    win_d = din("w_in", [D, c.INW]); sink_d = din("sinkB", [P, NQ])
    lng_d = din("ln_gT", [P, G]); lnb_d = din("ln_bT", [P, G])
    sguw_d = din("sgu_w", [G, P, P]); sgub_d = din("sgu_bB", [P, G * P])
    wps_d = din("w_proj_sgu", [SW, D]); wpa_d = din("w_proj_attn", [AW, D]); wout_d = din("w_out", [D, D])
    wg_d = din("w_gate", [D, c.HID]); wu_d = din("w_up", [D, c.HID]); wd_d = din("w_down", [c.HID, D])
    ident_d = din("identF", [P, P]); rt_d = din("rotT", [P, P]); band_d = din("maskB", [P, 256])
    mask0_d = din("mask0", [P, 256]); tri_d = din("tri", [P, P]); invf_d = din("invf", [32, 1])
    out_d = nc.dram_tensor("out", [TOK, D], F32, kind="ExternalOutput").ap()
    ggd_d = nc.dram_tensor("ggd", [2 * P, D], F32, kind="Internal").ap()

    YB = max(NCH * D * 4, 65536 if D >= 4096 else 0)
    off = 0
    def take(n):
        nonlocal off
        o = off; off += (n + 31) // 32 * 32
        return o
    o_vnq = take(max(NCH * SW * 2, NQ * TT * 2))
    o_k = take(NKV * 640 * 2)
    o_v = take(5 * KVW * 2)
    o_cs = take(2 * 640 * 4)
    o_scr = take(14 * 1024)
    o_oT = take(NQ * TT * 2)
    YB = max(YB, off)
    HB_ = KC * TT * 2
    MB = max(KC * TT * 2, D * 2 + D * 4, 2 * 8 * TT * 2)
    UB = max(G * TT * 2, 16384)

    def sb(name, nbytes):
        return es.enter_context(nc.sbuf_tensor(name, [P, (nbytes + 1) // 2], BF16))
    Y_t = sb("Y", YB); H_t = sb("H", HB_); M_t = sb("M", MB); U_t = sb("U", UB)
    R_t = sb("R", NSLOT * SLOT_ELEMS * 2)
    C_t = sb("C", 24 * 1024)
    tens = {"Y": Y_t, "H": H_t, "M": M_t, "U": U_t, "R": R_t, "C": C_t}

    def V(region, lo, shape, dt, name="", npart=P):
        esz = 2 if dt == BF16 else 4
        n = int(np.prod(shape))
        a = tens[region][0:npart, lo // 2:(lo + n * esz) // 2]
        if dt != BF16:
            a = a.bitcast(dt)
        if len(shape) == 2:
            a = a.rearrange("p (a b) -> p a b", b=shape[1])
        elif len(shape) == 3:
            a = a.rearrange("p (a b c) -> p a b c", b=shape[1], c=shape[2])
        return TV(a, S.buf(region, lo, lo + n * esz, name))

    def sub(tv, lo_bytes, nbytes, name=""):
        return S.buf(tv.b.region, tv.b.lo + lo_bytes, tv.b.lo + lo_bytes + nbytes, name)

    coff = 0
    def ctake(shape, dt, name, npart=P):
        nonlocal coff
        esz = 2 if dt == BF16 else 4
        n = int(np.prod(shape)) * esz
        t = V("C", coff, shape, dt, name, npart)
        coff += (n + 31) // 32 * 32
        return t
    identF = ctake([P], F32, "identF"); identB = ctake([P], BF16, "identB"); onesF = ctake([P], F32, "onesF")
    rotB = ctake([P], BF16, "rotB"); maskB = ctake([256], F32, "maskB"); mask0 = ctake([256], F32, "mask0")
    invf = ctake([1], F32, "invf", 32); sinkB = ctake([NQ], F32, "sinkB"); nsink = ctake([NQ], F32, "nsink")
    lng = ctake([G], F32, "lng"); lnb = ctake([G], F32, "lnb")
    WmT = ctake([G, P], BF16, "WmT"); bias2 = ctake([G, P], F32, "bias2")
    cT = ctake([KC], F32, "cT"); cact = ctake([KC], F32, "cact"); cactB = ctake([KC], BF16, "cactB")
    bada = ctake([6 * KC], F32, "bada"); modT = ctake([6 * KC], F32, "modT")
    gpm = ctake([KC], F32, "gpm"); gpo = ctake([KC], F32, "gpo"); gpf = ctake([KC], F32, "gpf"); gpof = ctake([KC], F32, "gpof")
    a1 = ctake([KC], F32, "a1"); a2 = ctake([KC], F32, "a2"); gg1 = ctake([KC], F32, "gg1"); gg2 = ctake([KC], F32, "gg2")
    epsR = ctake([1], F32, "epsR"); epsL = ctake([1], F32, "epsL")
    ssq = ctake([8], F32, "ssq"); rstd = ctake([8], F32, "rstd")
    ssqp = ctake([NCH, NCG], F32, "ssqp")
    stats = ctake([NCH, c.NSG * 6], F32, "stats"); mv = ctake([NCH, 2], F32, "mv")
    lnr = ctake([NCH], F32, "lnr"); lnm = ctake([NCH], F32, "lnm")
    mx = ctake([8], F32, "mx"); negm = ctake([8], F32, "negm"); rs = ctake([8], F32, "rs"); esk = ctake([8], F32, "esk")
    den = ctake([8], F32, "den")
    khalo = ctake([NKV, P], BF16, "khalo"); vhalo = ctake([KVW], BF16, "vhalo")
    assert coff <= 24 * 1024, coff

    PS = []
    for i in range(8):
        t = es.enter_context(nc.psum_tensor("ps%d" % i, [P, 512], F32))
        PS.append(TV(t[:, :], S.buf("PS", i * 2048, (i + 1) * 2048, "ps%d" % i)))

    slots = [V("R", i * SLOT_ELEMS * 2, [SLOT_ELEMS], BF16, "slot%d" % i) for i in range(NSLOT)]
    slot_ctr = [0]

    def load_slab(W, k0, nk, c0, ncols):
        assert nk * ncols <= SLOT_ELEMS
        i = slot_ctr[0] % NSLOT; slot_ctr[0] += 1
        sl = slots[i]
        dst = sl.ap[:, 0:nk * ncols].rearrange("p (a b) -> p a b", b=ncols)
        src = W[k0 * P:(k0 + nk) * P, c0:c0 + ncols].rearrange("(a p) n -> p a n", p=P)
        S.dma("pool", ("slot", i), lambda: nc.gpsimd.dma_start(out=dst, in_=src), reads=[], writes=[sl.b])
        return dst, sl.b

    def ckpt(n):
        if stop is not None and n == stop:
            raise _Stop()

    out_b = [[[S.buf("OUT", ((ti * NCH + ch) * NCG + cg), ((ti * NCH + ch) * NCG + cg) + 1) for cg in range(NCG)]
              for ch in range(NCH)] for ti in range(NT)]
    try:
        def cload(tv, src):
            S.dma("sp", "const", lambda: nc.sync.dma_start(out=tv.ap, in_=src), reads=[], writes=[tv.b])
        cload(identF, ident_d); cload(maskB, band_d); cload(mask0, mask0_d)
        cload(invf, invf_d); cload(sinkB, sink_d); cload(lng, lng_d); cload(lnb, lnb_d)
        cload(cT, cT_d); cload(bada, bada_d); cload(gpm, gpm_d); cload(gpo, gpo_d); cload(gpf, gpf_d); cload(gpof, gpof_d)
        sguw = V("Y", 0, [G, P], F32, "sguw")
        sgub = V("Y", G * P * 4, [G, P], F32, "sgub")
        tri = V("Y", 2 * G * P * 4, [P], F32, "tri")
        wmf = V("Y", 2 * G * P * 4 + 512, [P], F32, "wmf")
        S.dma("sp", "const", lambda: nc.sync.dma_start(out=sguw.ap, in_=sguw_d.rearrange("g t s -> t g s")), writes=[sguw.b])
        cload(sgub, sgub_d.rearrange("p (g t) -> p g t", t=P)); cload(tri, tri_d)
        rotF = V("Y", 2 * G * P * 4 + 1024, [P], F32, "rotF")
        cload(rotF, rt_d)
        csem, ctot = S.dsems["const"]
        for r in S.regions.values():
            for b in r:
                if b.last_w is not None and b.last_w[2] == ("dma", "const"):
                    b.last_w = (csem, ctot, ("dma", "const"))

        dve = nc.vector; act = nc.scalar; pe = nc.tensor

        S.op("dve", lambda: dve.tensor_copy(out=identB.ap, in_=identF.ap), [identF.b], [identB.b])
        S.op("dve", lambda: dve.tensor_copy(out=rotB.ap, in_=rotF.ap), [rotF.b], [rotB.b])
        S.op("dve", lambda: dve.memset(onesF.ap, 1.0), [], [onesF.b])
        S.op("dve", lambda: dve.memset(epsR.ap, RMS_EPS), [], [epsR.b])
        S.op("dve", lambda: dve.memset(epsL.ap, LN_EPS), [], [epsL.b])
        S.op("dve", lambda: dve.tensor_scalar(out=nsink.ap, in0=sinkB.ap, scalar1=-1.0, scalar2=None, op0=ALU.mult),
             [sinkB.b], [nsink.b])
        S.op("act", lambda: act.activation(out=cact.ap, in_=cT.ap, func=AF.Silu), [cT.b], [cact.b])
        ckpt(1)

        for g in range(G):
            pa = PS[(2 * g) % 8]; pb = PS[(2 * g + 1) % 8]
            S.op("pe", lambda: pe.transpose(out=pa.ap[:, 0:P], in_=sguw.ap[:, g, :], identity=identF.ap),
                 [sguw.b, identF.b], [pa.b])
            S.op("dve", lambda: dve.tensor_tensor(out=wmf.ap, in0=pa.ap[:, 0:P], in1=tri.ap, op=ALU.mult),
                 [pa.b, tri.b], [wmf.b])
            S.op("act", lambda: act.copy(out=WmT.ap[:, g, :], in_=wmf.ap), [wmf.b], [WmT.b])
            S.op("pe", lambda: pe.matmul(pb.ap[:, 0:P], lhsT=onesF.ap, rhs=wmf.ap, start=True, stop=True),
                 [onesF.b, wmf.b], [pb.b])
            S.op("dve", lambda: dve.scalar_tensor_tensor(out=bias2.ap[:, g, :], in0=pb.ap[:, 0:P], scalar=lnb.ap[:, g:g + 1],
                                                         in1=sgub.ap[:, g, :], op0=ALU.mult, op1=ALU.add),
                 [pb.b, lnb.b, sgub.b], [bias2.b])

        ckpt(2)
        S.op("dve", lambda: dve.tensor_copy(out=cactB.ap, in_=cact.ap), [cact.b], [cactB.b])
        NQD = 6 * KC // 4
        nkA = min(KC, SLOT_ELEMS // 512)
        psmq = [PS[4 + cc] for cc in range(4)]
        for qd in range(NQD):
            for s0 in range(0, KC, nkA):
                nke = min(nkA, KC - s0)
                slab, slb = load_slab(wada_d, s0, nke, qd * 512, 512)

                def emit():
                    ins = None
                    for cc in range(4):
                        for kl in range(nke):
                            kc = s0 + kl
                            ins = pe.matmul(psmq[cc].ap[:, qd:qd + 1], lhsT=slab[:, kl, cc * P:(cc + 1) * P], rhs=cactB.ap[:, kc:kc + 1],
                                            start=(kc == 0), stop=(kc == KC - 1))
                    return ins
                S.op("pe", emit, [slb, cactB.b], [b_.b for b_ in psmq])
        modT_v = modT.ap.rearrange("p (q c) -> p q c", c=4)
        bada_v = bada.ap.rearrange("p (q c) -> p q c", c=4)
        for cc in range(4):
            S.op("dve", lambda: dve.tensor_tensor(out=modT_v[:, :, cc], in0=psmq[cc].ap[:, 0:NQD], in1=bada_v[:, :, cc], op=ALU.add),
                 [psmq[cc].b, bada.b], [modT.b])
        def mk_a(dst, gvec, sc_lo):
            S.op("dve", lambda: dve.scalar_tensor_tensor(out=dst.ap, in0=modT.ap[:, sc_lo:sc_lo + KC], scalar=1.0, in1=gvec.ap,
                                                         op0=ALU.add, op1=ALU.mult), [modT.b, gvec.b], [dst.b])
        mk_a(a1, gpm, KC); mk_a(a2, gpf, 4 * KC)
        S.op("dve", lambda: dve.tensor_tensor(out=gg1.ap, in0=modT.ap[:, 2 * KC:3 * KC], in1=gpo.ap, op=ALU.mult),
             [modT.b, gpo.b], [gg1.b])
        S.op("dve", lambda: dve.tensor_tensor(out=gg2.ap, in0=modT.ap[:, 5 * KC:6 * KC], in1=gpof.ap, op=ALU.mult),
             [modT.b, gpof.b], [gg2.b])
        sh1 = TV(modT.ap[:, 0:KC], modT.b); sh2 = TV(modT.ap[:, 3 * KC:4 * KC], modT.b)
        ckpt(3)

        Ycc = [[V("Y", (ch * D + cg * 512) * 4, [512], F32, "Y%d_%d" % (ch, cg)) for cg in range(NCG)] for ch in range(NCH)]
        Ych = [tens["Y"][:, ch * D * 2:(ch + 1) * D * 2].bitcast(F32) for ch in range(NCH)]
        def Yb(ch):
            return [t.b for t in Ycc[ch]]
        hT = V("H", 0, [KC, TT], BF16, "hT")
        mergedT = V("M", 0, [KC, TT], BF16, "mergedT")
        junk = V("M", 0, [D], BF16, "junk")
        xhalo = V("M", D * 2, [D], F32, "xhalo")
        hTh = V("U", 0, [KC, P], BF16, "hTh")
        u_ap = tens["U"][:, 0:G * TT].rearrange("p (g t) -> p g t", t=TT)
        u_b = [S.buf("U", g * TT * 2, (g + 1) * TT * 2, "u%d" % g) for g in range(G)]
        xP = [V("U", i * 2048, [512], F32, "xP%d" % i) for i in range(4)]
        ggP = [V("U", 8192 + i * 2048, [512], F32, "ggP%d" % i) for i in range(2)]
        junk2 = V("U", 12288, [512], BF16, "junk2")
        rep = V("U", 13312, [P], F32, "rep")
        vn_ap = tens["Y"][:, o_vnq // 2:o_vnq // 2 + NCH * SW].rearrange("p (c s) -> p c s", s=SW)
        vn_b = [S.buf("Y", o_vnq + ch * SW * 2, o_vnq + (ch + 1) * SW * 2, "vn%d" % ch) for ch in range(NCH)]
        qT_ap = tens["Y"][:, o_vnq // 2:o_vnq // 2 + NQ * TT].rearrange("p (h t) -> p h t", t=TT)
        qT_b = [S.buf("Y", o_vnq + h * TT * 2, o_vnq + (h + 1) * TT * 2, "q%d" % h) for h in range(NQ)]
        kT_ap = tens["Y"][:, o_k // 2:o_k // 2 + NKV * 640].rearrange("p (h t) -> p h t", t=640)
        kT_b = [S.buf("Y", o_k + h * 1280, o_k + (h + 1) * 1280, "k%d" % h) for h in range(NKV)]
        v_ap = tens["Y"][:, o_v // 2:o_v // 2 + 5 * KVW].rearrange("p (c w) -> p c w", w=KVW)
        v_b = S.buf("Y", o_v, o_v + 5 * KVW * 2, "v")
        Ccs = V("Y", o_cs, [640], F32, "Ccs", 32); Scs = V("Y", o_cs + 2560, [640], F32, "Scs", 32)
        sm2 = [V("Y", o_scr + i * 4096, [4, 256], F32, "sm%d" % i) for i in range(2)]
        pn2 = [V("Y", o_scr + 8192 + i * 2048, [4, 256], BF16, "pn%d" % i) for i in range(2)]
        pT = V("Y", o_scr + 12288, [8, P], BF16, "pT")
        gtmp = [V("Y", o_scr + i * 2048, [512], F32, "gtmp%d" % i) for i in range(2)]
        stmp = V("Y", o_scr + 4096, [512], F32, "stmp")
        ropef = V("Y", o_scr, [512], F32, "ropef", 32); ropet = V("Y", o_scr + 2048, [512], F32, "ropet", 32)
        ang = V("Y", o_scr + 4096, [640], F32, "ang", 32); angi = V("Y", o_scr + 8192, [640], I32, "angi", 32)
        sga = V("Y", o_scr, [2, 512], F32, "sga"); sgb = V("Y", o_scr + 4096, [2, 512], F32, "sgb")
        mt1 = V("Y", o_scr + 8192, [2, 512], F32, "mt1")
        oT_ap = tens["Y"][:, o_oT // 2:o_oT // 2 + NQ * TT].rearrange("p (h t) -> p h t", t=TT)
        oT_b = [S.buf("Y", o_oT + hk * 4 * TT * 2, o_oT + (hk + 1) * 4 * TT * 2, "oT%d" % hk) for hk in range(NKV)]
        actb = [V("M", i * 8 * TT * 2, [8, TT], BF16, "actb%d" % i) for i in range(2)]
        sgt = [V("U", i * 2048, [512], F32, "sgt%d" % i) for i in range(2)]

        SCALE = float(128 ** -0.5)
        evac_rr = [0]

        def fm_group(W, col0, ncols, Kc, rhs_fn, rhs_bufs, banks, N=TT, rhs2_fn=None, rhs2_bufs=(), banks2=None, N2=P, krow0=0):
            nk = min(Kc, SLOT_ELEMS // ncols)
            ncc = ncols // P
            for s0 in range(0, Kc, nk):
                nke = min(nk, Kc - s0)
                slab, slb = load_slab(W, krow0 + s0, nke, col0, ncols)

                def emit():
                    ins = None
                    for cc in range(ncc):
                        for kl in range(nke):
                            kc = s0 + kl
                            ins = pe.matmul(banks[cc].ap[:, 0:N], lhsT=slab[:, kl, cc * P:(cc + 1) * P], rhs=rhs_fn(kc),
                                            start=(kc == 0), stop=(kc == Kc - 1))
                        if rhs2_fn is not None:
                            for kl in range(nke):
                                kc = s0 + kl
                                ins = pe.matmul(banks2[cc].ap[:, 0:N2], lhsT=slab[:, kl, cc * P:(cc + 1) * P], rhs=rhs2_fn(kc),
                                                start=(kc == 0), stop=(kc == Kc - 1))
                    return ins
                wr = [b.b for b in banks[:ncc]] + ([b.b for b in banks2[:ncc]] if rhs2_fn is not None else [])
                S.op("pe", emit, [slb] + list(rhs_bufs) + list(rhs2_bufs), wr)

        def tm_group(W, krow0, Kc, col0, ncols, lhs_fn, lhs_bufs, banks, chunks=range(NCH), lhs2_fn=None, lhs2_bufs=(), bank2=None):
            nk = min(Kc, SLOT_ELEMS // ncols)
            for s0 in range(0, Kc, nk):
                nke = min(nk, Kc - s0)
                slab, slb = load_slab(W, krow0 + s0, nke, col0, ncols)

                for ch in chunks:
                    def emit():
                        ins = None
                        for kl in range(nke):
                            kc = s0 + kl
                            ins = pe.matmul(banks[ch].ap[:, 0:ncols], lhsT=lhs_fn(kc, ch), rhs=slab[:, kl, :],
                                            start=(kc == 0), stop=(kc == Kc - 1))
                        return ins
                    S.op("pe", emit, [slb] + list(lhs_bufs), [banks[ch].b])
                if lhs2_fn is not None:
                    def emit():
                        ins = None
                        for kl in range(nke):
                            kc = s0 + kl
                            ins = pe.matmul(bank2.ap[:, 0:ncols], lhsT=lhs2_fn(kc), rhs=slab[:, kl, :],
                                            start=(kc == 0), stop=(kc == Kc - 1))
                        return ins
                    S.op("pe", emit, [slb] + list(lhs2_bufs), [bank2.b])

        def build_h(src_ap, src_bufs, dst_ap, dst_buf, avec, bvec, sidx, tcol0, ncols_tok=P):
            sq = TV(ssq.ap[:, sidx:sidx + 1], ssq.b); rsd = TV(rstd.ap[:, sidx:sidx + 1], rstd.b)
            S.op("act", lambda: act.activation(out=junk.ap, in_=src_ap, func=AF.Square, accum_out=sq.ap),
                 list(src_bufs), [junk.b, sq.b])
            S.op("act", lambda: act.activation(out=rsd.ap, in_=sq.ap, func=AF.Sqrt, bias=epsR.ap, scale=1.0 / D), [sq.b, epsR.b], [rsd.b])
            S.op("dve", lambda: dve.reciprocal(out=rsd.ap, in_=rsd.ap), [rsd.b], [rsd.b])
            S.op("dve", lambda: dve.tensor_scalar(out=src_ap, in0=src_ap, scalar1=rsd.ap, scalar2=None, op0=ALU.mult),
                 list(src_bufs) + [rsd.b], list(src_bufs))
            for kg in range(KC // 4):
                pb = PS[kg % 2]

                def emit():
                    ins = None
                    for j in range(4):
                        kc = kg * 4 + j
                        ins = pe.transpose(out=pb.ap[:, j * P:(j + 1) * P], in_=src_ap[:, kc * P:(kc + 1) * P], identity=identF.ap)
                    return ins
                S.op("pe", emit, list(src_bufs) + [identF.b], [pb.b])
                for j in range(4):
                    kc = kg * 4 + j
                    o_ = dst_ap[:, kc, tcol0:tcol0 + P]
                    i_ = pb.ap[:, j * P:(j + 1) * P]
                    if kg % 2 == 0:
                        S.op("act", lambda: act.activation(out=o_, in_=i_, func=AF.Identity, bias=bvec.ap[:, kc:kc + 1],
                                                           scale=avec.ap[:, kc:kc + 1]), [pb.b, avec.b, bvec.b], [dst_buf])
                    else:
                        S.op("dve", lambda: dve.tensor_scalar(out=o_, in0=i_, scalar1=avec.ap[:, kc:kc + 1], scalar2=bvec.ap[:, kc:kc + 1],
                                                              op0=ALU.mult, op1=ALU.add), [pb.b, avec.b, bvec.b], [dst_buf])

        ssq_c = [S.buf("C", ssq.b.lo + 4 * i, ssq.b.lo + 4 * i + 4, "ssq%d" % i) for i in range(8)]
        rstd_c = [S.buf("C", rstd.b.lo + 4 * i, rstd.b.lo + 4 * i + 4, "rstd%d" % i) for i in range(8)]
        hTk = [S.buf("H", kc * TT * 2, (kc + 1) * TT * 2, "hT%d" % kc) for kc in range(KC)]

        def build_h_tile(avec, bvec):
            for ch in range(NCH):
                sqa = ssq.ap[:, ch:ch + 1]; rsa = rstd.ap[:, ch:ch + 1]
                S.op("act", lambda: act.activation(out=junk.ap, in_=Ych[ch], func=AF.Square, accum_out=sqa), Yb(ch), [junk.b, ssq_c[ch]])
                S.op("act", lambda: act.activation(out=rsa, in_=sqa, func=AF.Sqrt, bias=epsR.ap, scale=1.0 / D), [ssq_c[ch], epsR.b], [rstd_c[ch]])
                S.op("dve", lambda: dve.reciprocal(out=rsa, in_=rsa), [rstd_c[ch]], [rstd_c[ch]])
                S.op("dve", lambda: dve.tensor_scalar(out=Ych[ch], in0=Ych[ch], scalar1=rsa, scalar2=None, op0=ALU.mult),
                     Yb(ch) + [rstd_c[ch]], Yb(ch))
            allY = [b_ for ch in range(NCH) for b_ in Yb(ch)]
            for kc in range(KC):
                pb = PS[kc % 4]

                def emit():
                    ins = None
                    for ch in range(NCH):
                        ins = pe.transpose(out=pb.ap[:, ch * P:(ch + 1) * P], in_=Ych[ch][:, kc * P:(kc + 1) * P], identity=identF.ap)
                    return ins
                S.op("pe", emit, allY + [identF.b], [pb.b])
                if kc % 2 == 0:
                    S.op("act", lambda: act.activation(out=hT.ap[:, kc, :], in_=pb.ap, func=AF.Identity, bias=bvec.ap[:, kc:kc + 1],
                                                       scale=avec.ap[:, kc:kc + 1]), [pb.b, avec.b, bvec.b], [hTk[kc]])
                else:
                    S.op("dve", lambda: dve.tensor_scalar(out=hT.ap[:, kc, :], in0=pb.ap, scalar1=avec.ap[:, kc:kc + 1], scalar2=bvec.ap[:, kc:kc + 1],
                                                          op0=ALU.mult, op1=ALU.add), [pb.b, avec.b, bvec.b], [hTk[kc]])

        def rope_evac(bank, dstT_ap, dst_buf, col0, N, cs_col0, rbank):
            S.op("act", lambda: act.copy(out=dstT_ap[:, col0:col0 + N], in_=bank.ap[:, 0:N]), [bank.b], [dst_buf])
            S.op("pe", lambda: pe.matmul(rbank.ap[:, 0:N], lhsT=rotB.ap, rhs=dstT_ap[:, col0:col0 + N], start=True, stop=True),
                 [rotB.b, dst_buf], [rbank.b])
            S.op("dve", lambda: dve.tensor_tensor(out=ropet.ap[:, 0:N], in0=rbank.ap[0:32, 0:N], in1=Scs.ap[:, cs_col0:cs_col0 + N], op=ALU.mult),
                 [rbank.b, Scs.b], [ropet.b])
            S.op("dve", lambda: dve.tensor_tensor(out=ropef.ap[:, 0:N], in0=bank.ap[0:32, 0:N], in1=Ccs.ap[:, cs_col0:cs_col0 + N], op=ALU.mult),
                 [bank.b, Ccs.b], [ropef.b])
            S.op("dve", lambda: dve.tensor_tensor(out=dstT_ap[0:32, col0:col0 + N], in0=ropef.ap[:, 0:N], in1=ropet.ap[:, 0:N], op=ALU.add),
                 [ropef.b, ropet.b], [dst_buf])

        def gen_ggP(ggvec, cg, dst):
            pb = PS[6 + (cg % 2)]
            for jj in range(4):
                col = cg * 4 + jj
                S.op("dve", lambda: dve.tensor_scalar(out=rep.ap, in0=onesF.ap, scalar1=ggvec.ap[:, col:col + 1], scalar2=None, op0=ALU.mult),
                     [onesF.b, ggvec.b], [rep.b])
                S.op("pe", lambda: pe.matmul(pb.ap[:, jj * P:(jj + 1) * P], lhsT=rep.ap, rhs=identF.ap, start=True, stop=True),
                     [rep.b, identF.b], [pb.b])
            S.op("act", lambda: act.copy(out=dst.ap, in_=pb.ap), [pb.b], [dst.b])

        def combine(ti, gv, src_is_x):
            S.op("dve", lambda: dve.tensor_reduce(out=ssq.ap[:, 0:NCH], in_=ssqp.ap, axis=AX.X, op=ALU.add), [ssqp.b], [ssq.b])
            S.op("act", lambda: act.activation(out=rstd.ap[:, 0:NCH], in_=ssq.ap[:, 0:NCH], func=AF.Sqrt, bias=epsR.ap, scale=1.0 / D),
                 [ssq.b, epsR.b], [rstd.b])
            S.op("dve", lambda: dve.reciprocal(out=rstd.ap[:, 0:NCH], in_=rstd.ap[:, 0:NCH]), [rstd.b], [rstd.b])
            ckpt(132)
            items = [(cg, ch) for cg in range(NCG) for ch in range(NCH)]

            def issue_load(k):
                cg, ch = items[k]
                xp = xP[k % 4]
                r0 = ti * TT + ch * P
                src = (x_d if src_is_x else out_d)[r0:r0 + P, cg * 512:(cg + 1) * 512]
                S.dma("sp", ("xp", k % 4), lambda: nc.sync.dma_start(out=xp.ap, in_=src),
                      reads=([] if src_is_x else [out_b[ti][ch][cg]]), writes=[xp.b])
            def issue_gg(cg):
                gp_ = ggP[cg % 2]
                S.dma("sp", ("ggld", cg % 2), lambda: nc.sync.dma_start(out=gp_.ap, in_=ggd_d[gv * P:(gv + 1) * P, cg * 512:(cg + 1) * 512]),
                      reads=[ggd_b[gv][cg]], writes=[gp_.b])
            issue_gg(0)
            if NCG > 1:
                issue_gg(1)
            for k in range(min(3, len(items))):
                issue_load(k)
            for k, (cg, ch) in enumerate(items):
                gp = ggP[cg % 2]
                if ch == 0 and cg >= 1 and cg + 1 < NCG:
                    issue_gg(cg + 1)
                xp = xP[k % 4]
                r0 = ti * TT + ch * P
                yb = Ycc[ch][cg]
                ob = out_b[ti][ch][cg]
                S.op("dve", lambda: dve.scalar_tensor_tensor(out=yb.ap, in0=yb.ap, scalar=rstd.ap[:, ch:ch + 1], in1=gp.ap,
                                                             op0=ALU.mult, op1=ALU.mult), [yb.b, rstd.b, gp.b], [yb.b])
                S.op("dve", lambda: dve.tensor_tensor(out=yb.ap, in0=yb.ap, in1=xp.ap, op=ALU.add), [yb.b, xp.b], [yb.b])
                S.dma("sp", ("st", ch, cg), lambda: nc.sync.dma_start(out=out_d[r0:r0 + P, cg * 512:(cg + 1) * 512], in_=yb.ap),
                      reads=[yb.b], writes=[ob])
                ckpt(134)
                if k + 3 < len(items):
                    issue_load(k + 3)

        ggd_b = [[S.buf("GGD", v * NCG + cg, v * NCG + cg + 1) for cg in range(NCG)] for v in range(2)]
        for v, ggvec in enumerate((gg1, gg2)):
            for cg in range(NCG):
                gp = ggP[cg % 2]
                gen_ggP(ggvec, cg, gp)
                S.dma("sp", ("ggst", cg % 2), lambda: nc.sync.dma_start(out=ggd_d[v * P:(v + 1) * P, cg * 512:(cg + 1) * 512], in_=gp.ap),
                      reads=[gp.b], writes=[ggd_b[v][cg]])

        for ti in range(NT):
            t0 = ti * TT
            if ti == 0:
                S.dma("sp", "xh", lambda: nc.sync.dma_start(out=xhalo.ap, in_=xh_d[:, :]), reads=[], writes=[xhalo.b])
                build_h(xhalo.ap, [xhalo.b], hTh.ap, hTh.b, a1, sh1, 4, 0)
            for ch in range(NCH):
                r0 = t0 + ch * P
                S.dma("sp", ("xl", ch), lambda: nc.sync.dma_start(out=Ych[ch], in_=x_d[r0:r0 + P, :]), reads=[], writes=Yb(ch))
            build_h_tile(a1, sh1)

            if ti == 0: ckpt(4)
            S.dma("sp", "pos", lambda: nc.sync.dma_start(out=angi.ap, in_=pos_d[:, t0:t0 + 640]),
                  reads=[], writes=[angi.b])
            S.op("dve", lambda: dve.tensor_copy(out=ang.ap, in_=angi.ap), [angi.b], [ang.b])
            S.op("dve", lambda: dve.tensor_scalar(out=ang.ap, in0=ang.ap, scalar1=invf.ap, scalar2=None, op0=ALU.mult), [ang.b, invf.b], [ang.b])
            S.op("dve", lambda: dve.tensor_scalar(out=angi.ap, in0=ang.ap, scalar1=float(1.0 / (2 * np.pi)), scalar2=None, op0=ALU.mult), [ang.b], [angi.b])
            S.op("dve", lambda: dve.tensor_copy(out=Scs.ap, in_=angi.ap), [angi.b], [Scs.b])
            S.op("dve", lambda: dve.scalar_tensor_tensor(out=ang.ap, in0=Scs.ap, scalar=float(-2 * np.pi), in1=ang.ap, op0=ALU.mult, op1=ALU.add),
                 [Scs.b, ang.b], [ang.b])
            S.op("act", lambda: act.activation(out=Scs.ap, in_=ang.ap, func=AF.Sin, scale=0.5), [ang.b], [Scs.b])
            S.op("act", lambda: act.activation(out=Ccs.ap, in_=ang.ap, func=AF.Sin, scale=0.25), [ang.b], [Ccs.b])
            S.op("dve", lambda: dve.tensor_tensor(out=Ccs.ap, in0=Ccs.ap, in1=Ccs.ap, op=ALU.mult), [Ccs.b], [Ccs.b])
            S.op("dve", lambda: dve.tensor_scalar(out=Ccs.ap, in0=Ccs.ap, scalar1=-2.0, scalar2=1.0, op0=ALU.mult, op1=ALU.add), [Ccs.b], [Ccs.b])
            S.op("dve", lambda: dve.tensor_tensor(out=ang.ap, in0=Scs.ap, in1=Scs.ap, op=ALU.mult), [Scs.b], [ang.b])
            S.op("dve", lambda: dve.scalar_tensor_tensor(out=Scs.ap, in0=Scs.ap, scalar=2.0, in1=Ccs.ap, op0=ALU.mult, op1=ALU.mult),
                 [Scs.b, Ccs.b], [Scs.b])
            S.op("dve", lambda: dve.tensor_scalar(out=Ccs.ap, in0=ang.ap, scalar1=-2.0, scalar2=1.0, op0=ALU.mult, op1=ALU.add), [ang.b], [Ccs.b])

            if ti == 0: ckpt(5)
            hT_rhs = lambda kc: hT.ap[:, kc, :]
            if ti > 0:
                for hk in range(NKV):
                    S.op("dve", lambda: dve.tensor_copy(out=kT_ap[:, hk, 0:P], in_=khalo.ap[:, hk, :]), [khalo.b], [kT_b[hk]])
                S.op("dve", lambda: dve.tensor_copy(out=v_ap[:, 0, :], in_=vhalo.ap), [vhalo.b], [v_b])
            kstep = 2 if NKV >= 2 else 1
            for hk0 in range(0, NKV, kstep):
                banks = [PS[2 + i] for i in range(kstep)]; banks2 = [PS[4 + i] for i in range(kstep)]
                fm_group(win_d, c.ok + hk0 * P, kstep * P, KC, hT_rhs, [hT.b], banks,
                         rhs2_fn=(lambda kc: hTh.ap[:, kc, :]) if ti == 0 else None, rhs2_bufs=[hTh.b] if ti == 0 else (), banks2=banks2)
                if ti == 0: ckpt(51)
                for i in range(kstep):
                    hk = hk0 + i
                    rope_evac(banks[i], kT_ap[:, hk], kT_b[hk], P, TT, P, PS[6])
                    if ti == 0: ckpt(52)
                    if ti == 0:
                        rope_evac(banks2[i], kT_ap[:, hk], kT_b[hk], 0, P, 0, PS[7])
            if ti == 0: ckpt(6)
            vb = [PS[ch] for ch in range(NCH)]
            tm_group(win_d, 0, KC, c.ov, KVW, lambda kc, ch: hT.ap[:, kc, ch * P:(ch + 1) * P], [hT.b], vb,
                     lhs2_fn=(lambda kc: hTh.ap[:, kc, :]) if ti == 0 else None, lhs2_bufs=[hTh.b] if ti == 0 else (), bank2=PS[4])
            for ch in range(NCH):
                S.op("act" if ch % 2 == 0 else "dve",
                     (lambda: act.copy(out=v_ap[:, 1 + ch, :], in_=vb[ch].ap[:, 0:KVW])) if ch % 2 == 0 else
                     (lambda: dve.tensor_copy(out=v_ap[:, 1 + ch, :], in_=vb[ch].ap[:, 0:KVW])), [vb[ch].b], [v_b])
            if ti == 0:
                S.op("act", lambda: act.copy(out=v_ap[:, 0, :], in_=PS[4].ap[:, 0:KVW]), [PS[4].b], [v_b])
            if ti + 1 < NT:
                for hk in range(NKV):
                    S.op("dve", lambda: dve.tensor_copy(out=khalo.ap[:, hk, :], in_=kT_ap[:, hk, TT:TT + P]), [kT_b[hk]], [khalo.b])
                S.op("dve", lambda: dve.tensor_copy(out=vhalo.ap, in_=v_ap[:, 4, :]), [v_b], [vhalo.b])

            if ti == 0: ckpt(7)
            for sg in range(c.NSG):
                bk = [PS[4 * (sg % 2) + ch] for ch in range(NCH)]
                tm_group(win_d, 0, KC, c.osv + sg * c.SGW, c.SGW, lambda kc, ch: hT.ap[:, kc, ch * P:(ch + 1) * P], [hT.b], bk)
                for ch in range(NCH):
                    gt = gtmp[ch % 2]
                    S.op("act", lambda: act.activation(out=gt.ap[:, 0:c.SGW], in_=bk[ch].ap[:, 0:c.SGW], func=AF.Gelu), [bk[ch].b], [gt.b])
                    S.op("dve", lambda: dve.bn_stats(out=stats.ap[:, ch, sg * 6:(sg + 1) * 6], in_=gt.ap[:, 0:c.SGW]), [gt.b], [stats.b])
                    S.op("dve", lambda: dve.tensor_copy(out=vn_ap[:, ch, sg * c.SGW:(sg + 1) * c.SGW], in_=gt.ap[:, 0:c.SGW]), [gt.b], [vn_b[ch]])
            for ch in range(NCH):
                S.op("dve", lambda: dve.bn_aggr(out=mv.ap[:, ch, :], in_=stats.ap[:, ch, :]), [stats.b], [mv.b])
            S.op("act", lambda: act.activation(out=lnr.ap, in_=mv.ap[:, :, 1], func=AF.Sqrt, bias=epsL.ap, scale=1.0), [mv.b, epsL.b], [lnr.b])
            S.op("dve", lambda: dve.reciprocal(out=lnr.ap, in_=lnr.ap), [lnr.b], [lnr.b])
            S.op("dve", lambda: dve.scalar_tensor_tensor(out=lnm.ap, in0=mv.ap[:, :, 0], scalar=-1.0, in1=lnr.ap, op0=ALU.mult, op1=ALU.mult),
                 [mv.b, lnr.b], [lnm.b])
            for ch in range(NCH):
                S.op("dve", lambda: dve.tensor_scalar(out=vn_ap[:, ch, :], in0=vn_ap[:, ch, :], scalar1=lnr.ap[:, ch:ch + 1], scalar2=lnm.ap[:, ch:ch + 1],
                                                      op0=ALU.mult, op1=ALU.add), [vn_b[ch], lnr.b, lnm.b], [vn_b[ch]])

            if ti == 0: ckpt(8)
            for gp_ in range(G // 2):
                banks = [PS[2 * (gp_ % 4)], PS[2 * (gp_ % 4) + 1]]
                fm_group(win_d, c.osu + gp_ * 256, 256, KC, hT_rhs, [hT.b], banks)
                for i in range(2):
                    g = gp_ * 2 + i
                    S.op("act", lambda: act.activation(out=u_ap[:, g, :], in_=banks[i].ap, func=AF.Gelu), [banks[i].b], [u_b[g]])

            if ti == 0: ckpt(9)
            for g in range(G):
                pb = PS[g % 4]

                def emit():
                    ins = None
                    for ch in range(NCH):
                        ins = pe.matmul(pb.ap[:, ch * P:(ch + 1) * P], lhsT=vn_ap[:, ch, g * P:(g + 1) * P], rhs=WmT.ap[:, g, :], start=True, stop=True)
                    return ins
                S.op("pe", emit, vn_b + [WmT.b], [pb.b])
                S.op("dve", lambda: dve.scalar_tensor_tensor(out=stmp.ap.rearrange("p (c t) -> p c t", t=P), in0=pb.ap.rearrange("p (c t) -> p c t", t=P),
                                                             scalar=lng.ap[:, g:g + 1], in1=bias2.ap[:, g, :].unsqueeze(1).broadcast_to([P, NCH, P]),
                                                             op0=ALU.mult, op1=ALU.add), [pb.b, lng.b, bias2.b], [stmp.b])
                S.op("dve", lambda: dve.tensor_tensor(out=u_ap[:, g, :], in0=stmp.ap, in1=u_ap[:, g, :], op=ALU.mult), [stmp.b, u_b[g]], [u_b[g]])

            if ti == 0: ckpt(10)
            for hp in range(NQ // 2):
                banks = [PS[2 * (hp % 3)], PS[2 * (hp % 3) + 1]]
                fm_group(win_d, c.oq + hp * 256, 256, KC, hT_rhs, [hT.b], banks)
                for i in range(2):
                    h = hp * 2 + i
                    rope_evac(banks[i], qT_ap[:, h], qT_b[h], 0, TT, P, PS[6 + i])

            if ti == 0: ckpt(11)
            groups = [(b_, hk_) for b_ in range(NCH) for hk_ in range(NKV)]

            def att_banks(gi):
                par = gi % 2
                return [PS[par * 3 + 0], PS[par * 3 + 1]], PS[par * 3 + 2], PS[6]

            def att_scores(gi):
                b_, hk_ = groups[gi]
                ps_s, _, _ = att_banks(gi)

                def emit():
                    ins = None
                    for hh in range(4):
                        h = hk_ * 4 + hh
                        ins = pe.matmul(ps_s[hh // 2].ap[:, (hh % 2) * 256:(hh % 2) * 256 + 256], lhsT=qT_ap[:, h, b_ * P:(b_ + 1) * P],
                                        rhs=kT_ap[:, hk_, b_ * P:b_ * P + 256], start=True, stop=True)
                    return ins
                S.op("pe", emit, [qT_b[hk_ * 4 + hh] for hh in range(4)] + [kT_b[hk_]], [ps_s[0].b, ps_s[1].b])

            att_scores(0)
            for gi, (b, hk) in enumerate(groups):
                ps_s, ps_t, ps_o = att_banks(gi)
                smx = sm2[gi % 2]; pnx = pn2[gi % 2]
                msk = mask0 if (ti == 0 and b == 0) else maskB
                for h2 in range(2):
                    S.op("dve", lambda: dve.tensor_tensor(out=smx.ap[:, 2 * h2:2 * h2 + 2, :], in0=ps_s[h2].ap.rearrange("p (a k) -> p a k", k=256),
                                                          in1=msk.ap.unsqueeze(1).broadcast_to([P, 2, 256]), op=ALU.add), [ps_s[h2].b, msk.b], [smx.b])
                S.op("dve", lambda: dve.tensor_reduce(out=mx.ap[:, 0:4], in_=smx.ap, axis=AX.X, op=ALU.max), [smx.b], [mx.b])
                S.op("dve", lambda: dve.scalar_tensor_tensor(out=negm.ap[:, 0:4], in0=mx.ap[:, 0:4], scalar=-SCALE, in1=nsink.ap[:, hk * 4:hk * 4 + 4],
                                                             op0=ALU.mult, op1=ALU.min), [mx.b, nsink.b], [negm.b])
                S.op("dve", lambda: dve.tensor_tensor(out=esk.ap[:, 0:4], in0=negm.ap[:, 0:4], in1=sinkB.ap[:, hk * 4:hk * 4 + 4], op=ALU.add),
                     [negm.b, sinkB.b], [esk.b])
                if gi + 1 < len(groups):
                    att_scores(gi + 1)
                for hh in range(4):
                    h = hk * 4 + hh
                    S.op("act", lambda: act.activation(out=smx.ap[:, hh, :], in_=smx.ap[:, hh, :], func=AF.Exp, bias=negm.ap[:, hh:hh + 1], scale=SCALE,
                                                       accum_out=rs.ap[:, hh:hh + 1]), [smx.b, negm.b], [smx.b, rs.b])
                S.op("act", lambda: act.activation(out=esk.ap[:, 0:4], in_=esk.ap[:, 0:4], func=AF.Exp), [esk.b], [esk.b])
                S.op("dve", lambda: dve.tensor_tensor(out=den.ap[:, 0:4], in0=rs.ap[:, 0:4], in1=esk.ap[:, 0:4], op=ALU.add), [rs.b, esk.b], [den.b])
                S.op("dve", lambda: dve.reciprocal(out=den.ap[:, 0:4], in_=den.ap[:, 0:4]), [den.b], [den.b])
                for hh in range(4):
                    S.op("dve", lambda: dve.tensor_scalar(out=pnx.ap[:, hh, :], in0=smx.ap[:, hh, :], scalar1=den.ap[:, hh:hh + 1], scalar2=None, op0=ALU.mult),
                         [smx.b, den.b], [pnx.b])
                pst = ps_t.ap.bitcast(BF16)

                def emit():
                    ins = None
                    for hh in range(4):
                        for k2 in range(2):
                            ins = pe.transpose(out=pst[:, (hh * 2 + k2) * P:(hh * 2 + k2 + 1) * P], in_=pnx.ap[:, hh, k2 * P:(k2 + 1) * P], identity=identB.ap)
                    return ins
                S.op("pe", emit, [pnx.b, identB.b], [ps_t.b])
                S.op("act", lambda: act.copy(out=pT.ap.rearrange("p a b -> p (a b)"), in_=pst), [ps_t.b], [pT.b])

                def emit():
                    ins = None
                    for hh in range(4):
                        for k2 in range(2):
                            ins = pe.matmul(ps_o.ap[:, hh * P:(hh + 1) * P], lhsT=v_ap[:, b + k2, hk * P:(hk + 1) * P], rhs=pT.ap[:, hh * 2 + k2, :],
                                            start=(k2 == 0), stop=(k2 == 1))
                    return ins
                S.op("pe", emit, [v_b, pT.b], [ps_o.b])
                S.op("act", lambda: act.copy(out=oT_ap[:, hk * 4:hk * 4 + 4, b * P:(b + 1) * P], in_=ps_o.ap.rearrange("p (h q) -> p h q", q=P)),
                     [ps_o.b], [oT_b[hk]])

            if ti == 0: ckpt(12)
            for j2 in range(KC // 2):
                col0 = j2 * 256
                bA = [PS[0], PS[1]]; bB = [PS[2], PS[3]]; bS = [PS[4], PS[5]]; bT = [PS[6], PS[7]]
                fm_group(win_d, c.oga + col0, 256, KC, hT_rhs, [hT.b], bA)
                S.op("act", lambda: act.activation(out=sga.ap[:, 0, :], in_=bA[0].ap, func=AF.Sigmoid), [bA[0].b], [sga.b])
                S.op("act", lambda: act.activation(out=sga.ap[:, 1, :], in_=bA[1].ap, func=AF.Sigmoid), [bA[1].b], [sga.b])
                fm_group(win_d, c.ogb + col0, 256, KC, hT_rhs, [hT.b], bB)
                S.op("act", lambda: act.activation(out=sgb.ap[:, 0, :], in_=bB[0].ap, func=AF.Sigmoid), [bB[0].b], [sgb.b])
                S.op("act", lambda: act.activation(out=sgb.ap[:, 1, :], in_=bB[1].ap, func=AF.Sigmoid), [bB[1].b], [sgb.b])
                fm_group(wps_d, col0, 256, G, lambda kc: u_ap[:, kc, :], u_b, bS)
                for i in range(2):
                    S.op("dve", lambda: dve.tensor_tensor(out=mt1.ap[:, i, :], in0=bS[i].ap, in1=sga.ap[:, i, :], op=ALU.mult), [bS[i].b, sga.b], [mt1.b])
                fm_group(wpa_d, col0, 256, NQ, lambda kc: oT_ap[:, kc, :], oT_b, bT)
                for i in range(2):
                    S.op("dve", lambda: dve.tensor_tensor(out=sgb.ap[:, i, :], in0=bT[i].ap, in1=sgb.ap[:, i, :], op=ALU.mult), [bT[i].b, sgb.b], [sgb.b])
                S.op("dve", lambda: dve.tensor_tensor(out=mergedT.ap[:, j2 * 2:j2 * 2 + 2, :], in0=mt1.ap, in1=sgb.ap, op=ALU.add), [mt1.b, sgb.b], [mergedT.b])

            if ti == 0: ckpt(13)
            for cg in range(NCG):
                bk = [PS[4 * (cg % 2) + ch] for ch in range(NCH)]
                tm_group(wout_d, 0, KC, cg * 512, 512, lambda kc, ch: mergedT.ap[:, kc, ch * P:(ch + 1) * P], [mergedT.b], bk)
                for ch in range(NCH):
                    S.op("dve", lambda: dve.tensor_copy(out=Ycc[ch][cg].ap, in_=bk[ch].ap), [bk[ch].b], [Ycc[ch][cg].b])
                    S.op("act", lambda: act.activation(out=junk2.ap, in_=Ycc[ch][cg].ap, func=AF.Square, accum_out=ssqp.ap[:, ch, cg:cg + 1]),
                         [Ycc[ch][cg].b], [junk2.b, ssqp.b])
            if ti == 0: ckpt(131)
            combine(ti, 0, True)

            if ti == 0: ckpt(14)
            build_h_tile(a2, sh2)

            if ti == 0: ckpt(15)
            NHB = (HC + 7) // 8
            for hb in range(NHB):
                nch_ = min(8, HC - hb * 8)
                ab = actb[hb % 2]
                for pr in range(nch_ // 2):
                    col0 = (hb * 8 + pr * 2) * P
                    bG = [PS[0], PS[1]]; bU = [PS[2], PS[3]]
                    fm_group(wg_d, col0, 256, KC, hT_rhs, [hT.b], bG)
                    fm_group(wu_d, col0, 256, KC, hT_rhs, [hT.b], bU)
                    for i in range(2):
                        st_ = sgt[i]
                        S.op("act", lambda: act.activation(out=st_.ap, in_=bG[i].ap, func=AF.Silu), [bG[i].b], [st_.b])
                        S.op("dve", lambda: dve.tensor_tensor(out=ab.ap[:, pr * 2 + i, :], in0=st_.ap, in1=bU[i].ap, op=ALU.mult), [st_.b, bU[i].b], [ab.b])
                for cg in range(NCG):
                    bk = [PS[4 + ch] for ch in range(NCH)]
                    tm_group(wd_d, hb * 8, nch_, cg * 512, 512, lambda kc, ch: ab.ap[:, kc, ch * P:(ch + 1) * P], [ab.b], bk)
                    for ch in range(NCH):
                        yb = Ycc[ch][cg]
                        if hb == 0:
                            S.op("dve", lambda: dve.tensor_copy(out=yb.ap, in_=bk[ch].ap), [bk[ch].b], [yb.b])
                        else:
                            S.op("dve", lambda: dve.tensor_tensor(out=yb.ap, in0=bk[ch].ap, in1=yb.ap, op=ALU.add), [bk[ch].b, yb.b], [yb.b])
            for cg in range(NCG):
                for ch in range(NCH):
                    S.op("act", lambda: act.activation(out=junk2.ap, in_=Ycc[ch][cg].ap, func=AF.Square, accum_out=ssqp.ap[:, ch, cg:cg + 1]),
                         [Ycc[ch][cg].b], [junk2.b, ssqp.b])
            combine(ti, 1, False)

    except _Stop:
        S.dma('sp', ('st', 0, 0), lambda: nc.sync.dma_start(out=out_d[0:P, 0:512], in_=tens['C'][:, 0:1024].bitcast(F32)), reads=[], writes=[out_b[0][0][0]])
    allout = [out_b[ti][ch][cg] for ti in range(NT) for ch in range(NCH) for cg in range(NCG)]
    S.final_wait("sp", allout)
    es.close()
    return nc


def _consts():
    identF = np.eye(P, dtype=np.float32)
    rotT = np.zeros((P, P), np.float32)
    for m in range(16):
        rotT[m + 16, m] = -1.0
        rotT[m, m + 16] = 1.0
    qi = np.arange(P)[:, None]; ki = np.arange(256)[None, :]
    diff = qi + P - ki
    band = np.where((diff >= 0) & (diff < P), 0.0, NEG).astype(np.float32)
    s_ = np.arange(P)[:, None]; t_ = np.arange(P)[None, :]
    tri = (s_ <= t_).astype(np.float32)
    inv = (np.float32(ROPE_THETA) ** (-np.arange(0, 32, 2, dtype=np.float32) / np.float32(32))).astype(np.float32)
    invf = np.concatenate([inv, inv]).reshape(32, 1).astype(np.float32)
    return identF, rotT, band, tri, invf


def make_in_maps(cfg, inp):
    c = cfg
    f32 = np.float32
    identF, rotT, band, tri, invf = _consts()
    x = np.asarray(inp["x"], f32); cc = np.asarray(inp["c"], f32); pos = np.asarray(inp["positions"]).astype(np.int32)
    fm = lambda v, n: np.ascontiguousarray(np.asarray(v, f32).reshape(n, P).T)
    L = 0
    shared = {
        "w_ada": np.ascontiguousarray(np.asarray(inp["w_ada"], f32)[L]),
        "b_adaT": fm(np.asarray(inp["b_ada"])[L], 6 * c.KC),
        "g_pre_mixT": fm(np.asarray(inp["g_pre_mix"])[L], c.KC), "g_post_mixT": fm(np.asarray(inp["g_post_mix"])[L], c.KC),
        "g_pre_ffnT": fm(np.asarray(inp["g_pre_ffn"])[L], c.KC), "g_post_ffnT": fm(np.asarray(inp["g_post_ffn"])[L], c.KC),
        "w_in": np.ascontiguousarray(np.asarray(inp["w_in"], f32)[L]),
        "sinkB": np.ascontiguousarray(np.broadcast_to(np.asarray(inp["attn_sinks"], f32)[L][None, :], (P, c.NQ))),
        "ln_gT": fm(np.asarray(inp["sgu_ln_g"])[L], c.G), "ln_bT": fm(np.asarray(inp["sgu_ln_b"])[L], c.G),
        "sgu_w": np.ascontiguousarray(np.asarray(inp["sgu_w"], f32)[L]),
        "sgu_bB": np.ascontiguousarray(np.broadcast_to(np.asarray(inp["sgu_b"], f32)[L].reshape(1, -1), (P, c.G * P))),
        "w_proj_sgu": np.ascontiguousarray(np.asarray(inp["w_proj_sgu"], f32)[L]),
        "w_proj_attn": np.ascontiguousarray(np.asarray(inp["w_proj_attn"], f32)[L]),
        "w_out": np.ascontiguousarray(np.asarray(inp["w_out"], f32)[L]),
        "w_gate": np.ascontiguousarray(np.asarray(inp["w_gate"], f32)[L]),
        "w_up": np.ascontiguousarray(np.asarray(inp["w_up"], f32)[L]),
        "w_down": np.ascontiguousarray(np.asarray(inp["w_down"], f32)[L]),
        "identF": identF, "rotT": rotT, "maskB": band, "tri": tri, "invf": invf,
    }
    maps = []
    for i in range(c.NCORES):
        b = i // c.CPB; half = i % c.CPB; t0 = half * c.TOK
        m = dict(shared)
        m["x"] = np.ascontiguousarray(x[b, t0:t0 + c.TOK])
        if half > 0:
            m["xh"] = np.ascontiguousarray(x[b, t0 - P:t0])
            ph = pos[b, t0 - P:t0]
            m["mask0"] = band
        else:
            m["xh"] = np.zeros((P, c.D), f32)
            ph = np.zeros((P,), np.int32)
            mk = band.copy(); mk[:, 0:P] = NEG
            m["mask0"] = mk
        m["pos"] = np.ascontiguousarray(np.broadcast_to(np.concatenate([ph, pos[b, t0:t0 + c.TOK]]).reshape(1, -1).astype(np.int32), (32, c.TOK + P)))
        m["cT"] = fm(cc[b], c.KC)
        maps.append(m)
    return maps


def run(cfg, inp, trace=False, stop=None):
    nc = build_program(cfg, stop)
    maps = make_in_maps(cfg, inp)
    res = run_bass_kernel_spmd(nc, maps, core_ids=list(range(cfg.NCORES)), trace=trace)
    outs = [np.asarray(r["out"]) for r in res.results]
    B = cfg.BATCH
    full = np.stack([np.concatenate(outs[b * cfg.CPB:(b + 1) * cfg.CPB], axis=0) for b in range(B)], axis=0)
    return full.astype(np.float32), res


def kernel(**inputs):
    cfg = Cfg(4096, 4096, 4)
    out, _ = run(cfg, inputs)
    return out
```
